# Optimizing a Trainium2 kernel written in Bass

```python
import math
import jax, jax.numpy as jnp
from jax import lax
import numpy as np

D_MODEL = 1024
BATCH = 1
SEQ = 16384
DEPTH = 1

MIX_DIM = D_MODEL
RWKV_HEADS = 8
RWKV_HEAD_DIM = 64
RWKV_DIM = RWKV_HEADS * RWKV_HEAD_DIM
LORA_W = 64
LORA_A = 64
LORA_G = 128
DIFF_HEADS = 4
DIFF_HEAD_DIM = 64
DIFF_DIM = DIFF_HEADS * 2 * DIFF_HEAD_DIM
RWKV_COLS = 3 * RWKV_DIM + LORA_W + LORA_A + LORA_G
DIFF_COLS = 3 * DIFF_DIM
IN_COLS = RWKV_COLS + DIFF_COLS
Q_BLOCK = 128
D_FF = 2816
CONV_WIDTH = 3
DECAY_SCALE = math.exp(-0.5)
RWKV_LN_EPS = 64e-5
NORM_EPS = 1e-6
SUBLN_EPS = 1e-5
NEG_INF = -1e30

kernel_name = "hybrid_rwkv7_diffattn_convffn"


def rms_norm(x, g, eps=NORM_EPS):
    x32 = x.astype(jnp.float32)
    y = x32 * lax.rsqrt(jnp.mean(x32 * x32, axis=-1, keepdims=True) + eps)
    return (y * g.astype(jnp.float32)).astype(x.dtype)


def alibi_slopes(n):
    return jnp.array([2.0 ** (-8.0 * (i + 1) / n) for i in range(n)], jnp.float32)


def token_shift(p):
    return jnp.pad(p[:, :-1], ((0, 0), (1, 0), (0, 0)))


def rwkv7_mix(p, mu, w_decay_up, w_decay0, w_iclr_up, w_iclr0, w_gate_up,
              k_k, k_a, r_k, ln_x_w, ln_x_b):
    B, S, _ = p.shape
    p = p.astype(jnp.float32)
    xs = p + (token_shift(p) - p) * mu
    cuts = [RWKV_DIM, 2 * RWKV_DIM, 3 * RWKV_DIM, 3 * RWKV_DIM + LORA_W,
            3 * RWKV_DIM + LORA_W + LORA_A]
    r, k, v, w_lo, a_lo, g_lo = jnp.split(xs, cuts, axis=-1)
    decay = jnp.exp(-DECAY_SCALE * jax.nn.sigmoid(w_decay0 + jnp.tanh(w_lo) @ w_decay_up))
    a = jax.nn.sigmoid(w_iclr0 + a_lo @ w_iclr_up)
    g = jax.nn.sigmoid(g_lo) @ w_gate_up
    hs = (B, S, RWKV_HEADS, RWKV_HEAD_DIM)
    kk = (k * k_k).reshape(hs)
    kk = kk / jnp.maximum(jnp.linalg.norm(kk, axis=-1, keepdims=True), 1e-12)
    k = k * (1.0 + (a - 1.0) * k_a)
    r, k, v, decay, a = (t.reshape(hs) for t in (r, k, v, decay, a))

    def step(state, inp):
        r_t, w_t, k_t, v_t, kk_t, a_t = inp
        sa = jnp.einsum('bhvk,bhk->bhv', state, -kk_t)
        state = (state * w_t[:, :, None, :]
                 + sa[..., None] * (kk_t * a_t)[:, :, None, :]
                 + v_t[..., None] * k_t[:, :, None, :])
        return state, jnp.einsum('bhvk,bhk->bhv', state, r_t)

    seq_major = tuple(jnp.moveaxis(t, 1, 0) for t in (r, decay, k, v, kk, a))
    s0 = jnp.zeros((B, RWKV_HEADS, RWKV_HEAD_DIM, RWKV_HEAD_DIM), jnp.float32)
    _, y = lax.scan(step, s0, seq_major)
    y = jnp.moveaxis(y, 0, 1)
    y = y + jnp.sum(r * k * r_k, axis=-1, keepdims=True) * v
    mean = jnp.mean(y, axis=-1, keepdims=True)
    var = jnp.mean(jnp.square(y - mean), axis=-1, keepdims=True)
    y = ((y - mean) * lax.rsqrt(var + RWKV_LN_EPS)).reshape(B, S, RWKV_DIM)
    y = y * ln_x_w + ln_x_b
    return y * g


def diff_attention(q, k, v, lam, slopes):
    B, S, H, _, d = q.shape
    nb = S // Q_BLOCK
    scale = d ** -0.5
    qh = q.transpose(0, 2, 3, 1, 4)
    kh = k.transpose(0, 2, 3, 1, 4).astype(jnp.float32)
    vh = v.transpose(0, 2, 1, 3).astype(jnp.float32)
    qb = qh.reshape(B, H, 2, nb, Q_BLOCK, d).transpose(3, 0, 1, 2, 4, 5)
    kpos = jnp.arange(S)

    def block(args):
        qblk, i = args
        qpos = i * Q_BLOCK + jnp.arange(Q_BLOCK)
        dist = (qpos[:, None] - kpos[None, :]).astype(jnp.float32)
        s = jnp.einsum('bhcqd,bhckd->bhcqk', qblk.astype(jnp.float32), kh) * scale
        s = s - slopes[None, :, None, None, None] * dist
        s = jnp.where(dist >= 0, s, NEG_INF)
        pr = jax.nn.softmax(s, axis=-1)
        attn = pr[:, :, 0] - lam * pr[:, :, 1]
        return jnp.einsum('bhqk,bhkv->bhqv', attn, vh)

    out = lax.map(block, (qb, jnp.arange(nb)))
    return out.transpose(1, 0, 3, 2, 4).reshape(B, S, H, 2 * d)


def causal_dwconv(h, w, b):
    C = h.shape[-1]
    y = lax.conv_general_dilated(h, w[:, None, :].astype(h.dtype), window_strides=(1,),
                                 padding=((CONV_WIDTH - 1, 0),),
                                 dimension_numbers=('NWC', 'WIO', 'NWC'),
                                 feature_group_count=C)
    return y + b.astype(h.dtype)


def setup_inputs(seed: int = 0) -> dict:
    key = jax.random.key(seed)
    ks = jax.random.split(key, 32)
    L = DEPTH

    def nrm(k, shape, s):
        return jax.random.normal(k, shape, jnp.float32) * s

    return {
        "x": nrm(ks[0], (BATCH, SEQ, D_MODEL), 1.0),
        "ln_attn_pre": 1.0 + nrm(ks[1], (L, D_MODEL), 0.05),
        "w_in": nrm(ks[2], (L, D_MODEL, IN_COLS), D_MODEL ** -0.5),
        "mu_shift": jax.random.uniform(ks[3], (L, RWKV_COLS), jnp.float32),
        "w_decay_up": nrm(ks[4], (L, LORA_W, RWKV_DIM), LORA_W ** -0.5),
        "w_decay0": nrm(ks[5], (L, RWKV_DIM), 0.5),
        "w_iclr_up": nrm(ks[6], (L, LORA_A, RWKV_DIM), LORA_A ** -0.5),
        "w_iclr0": nrm(ks[7], (L, RWKV_DIM), 0.1),
        "w_gate_up": nrm(ks[8], (L, LORA_G, RWKV_DIM), LORA_G ** -0.5),
        "k_k": 0.85 + nrm(ks[9], (L, RWKV_DIM), 0.05),
        "k_a": 1.0 + nrm(ks[10], (L, RWKV_DIM), 0.05),
        "r_k": nrm(ks[11], (L, RWKV_HEADS, RWKV_HEAD_DIM), 0.1),
        "ln_x_w": 1.0 + nrm(ks[12], (L, RWKV_DIM), 0.05),
        "ln_x_b": nrm(ks[13], (L, RWKV_DIM), 0.02),
        "lambda_q1": nrm(ks[14], (L, DIFF_HEAD_DIM), 0.1),
        "lambda_k1": nrm(ks[15], (L, DIFF_HEAD_DIM), 0.1),
        "lambda_q2": nrm(ks[16], (L, DIFF_HEAD_DIM), 0.1),
        "lambda_k2": nrm(ks[17], (L, DIFF_HEAD_DIM), 0.1),
        "diff_subln": 1.0 + nrm(ks[18], (L, 2 * DIFF_HEAD_DIM), 0.05),
        "w_out": nrm(ks[19], (L, MIX_DIM, D_MODEL), MIX_DIM ** -0.5),
        "ln_attn_post": 1.0 + nrm(ks[20], (L, D_MODEL), 0.05),
        "ln_ffn_pre": 1.0 + nrm(ks[21], (L, D_MODEL), 0.05),
        "w_up": nrm(ks[22], (L, D_MODEL, 2 * D_FF), D_MODEL ** -0.5),
        "conv_w": nrm(ks[23], (L, CONV_WIDTH, 2 * D_FF), CONV_WIDTH ** -0.5),
        "conv_b": nrm(ks[24], (L, 2 * D_FF), 0.02),
        "w_down": nrm(ks[25], (L, D_FF, D_MODEL), D_FF ** -0.5),
        "ln_ffn_post": 1.0 + nrm(ks[26], (L, D_MODEL), 0.05),
    }


def reference(x, ln_attn_pre, w_in, mu_shift, w_decay_up, w_decay0, w_iclr_up, w_iclr0,
              w_gate_up, k_k, k_a, r_k, ln_x_w, ln_x_b, lambda_q1, lambda_k1, lambda_q2,
              lambda_k2, diff_subln, w_out, ln_attn_post, ln_ffn_pre, w_up, conv_w, conv_b,
              w_down, ln_ffn_post):
    B, S, _ = x.shape
    slopes = alibi_slopes(DIFF_HEADS)
    h = x
    for l in range(DEPTH):
        xn = rms_norm(h, ln_attn_pre[l])
        proj = xn @ w_in[l]
        y_rwkv = rwkv7_mix(proj[..., :RWKV_COLS], mu_shift[l], w_decay_up[l], w_decay0[l],
                           w_iclr_up[l], w_iclr0[l], w_gate_up[l], k_k[l], k_a[l], r_k[l],
                           ln_x_w[l], ln_x_b[l])
        dq, dk, dv = jnp.split(proj[..., RWKV_COLS:], 3, axis=-1)
        dq = dq.reshape(B, S, DIFF_HEADS, 2, DIFF_HEAD_DIM)
        dk = dk.reshape(B, S, DIFF_HEADS, 2, DIFF_HEAD_DIM)
        dv = dv.reshape(B, S, DIFF_HEADS, 2 * DIFF_HEAD_DIM)
        lam_init = 0.8 - 0.6 * math.exp(-0.3 * l)
        lam = (jnp.exp(jnp.sum(lambda_q1[l] * lambda_k1[l]).astype(jnp.float32))
               - jnp.exp(jnp.sum(lambda_q2[l] * lambda_k2[l]).astype(jnp.float32)) + lam_init)
        o = diff_attention(dq, dk, dv, lam, slopes)
        o = rms_norm(o, diff_subln[l], eps=SUBLN_EPS) * (1.0 - lam_init)
        mix = jnp.concatenate([y_rwkv, o.reshape(B, S, DIFF_DIM)], axis=-1).astype(h.dtype)
        h = h + rms_norm(mix @ w_out[l], ln_attn_post[l])
        hn = rms_norm(h, ln_ffn_pre[l])
        u = causal_dwconv(hn @ w_up[l], conv_w[l], conv_b[l])
        gate, val = jnp.split(u, 2, axis=-1)
        f = (jax.nn.gelu(gate, approximate=True) * val) @ w_down[l]
        h = h + rms_norm(f, ln_ffn_post[l])
    return h
```

```python
import math
import numpy as np
from contextlib import ExitStack
import concourse.bass as bass
import concourse.mybir as mybir
from concourse.bass_utils import run_bass_kernel_spmd

F32 = mybir.dt.float32
BF16 = mybir.dt.bfloat16
AF = mybir.ActivationFunctionType
ALU = mybir.AluOpType
AX = mybir.AxisListType

N_DMA_SEMS = 24


class Sched:
    ENGS = ("pe", "act", "dve", "pool", "sp")

    def __init__(self, nc, es):
        self.nc = nc
        self.es = es
        self.h = {"pe": nc.tensor, "act": nc.scalar, "dve": nc.vector, "pool": nc.gpsimd, "sp": nc.sync}
        self.ops = {k: [] for k in self.ENGS}
        self.cnt = {}
        self.sems = {}
        for k in self.ENGS:
            self.sems["e_" + k] = es.enter_context(nc.semaphore("e_" + k))
            self.cnt["e_" + k] = 0
        for i in range(N_DMA_SEMS):
            self.sems["d%d" % i] = es.enter_context(nc.semaphore("d%d" % i))
            self.cnt["d%d" % i] = 0
        self.dma_rr = 0
        self.last_w = {}
        self.readers = {}
        self.seen = {k: {} for k in self.ENGS}
        self.nwaits = 0
        self.alias = {}
        self.pe_skip_all = False
        self._skip_pe = False
        self.pe_plain = {}
        self.plain_chain = False
        self.exclusive = set()
        import os
        self.limit = int(os.environ.get('OPLIMIT', '0')) or None
        self.total = 0

    def _need(self, eng, ev, waits):
        if ev is None:
            return
        s, v = ev
        if eng == 'pe' and s == 'e_pe' and (self.pe_skip_all or (self._skip_pe and (not self.plain_chain or self.pe_plain.get(v, False)))):
            return
        if self.seen[eng].get(s, 0) >= v:
            return
        cur = waits.get(s, 0)
        if v > cur:
            waits[s] = v

    def op(self, eng, meth, kw, reads=(), writes=(), dma=False, skip_pe=False):
        self._skip_pe = skip_pe
        self.total += 1
        if self.limit is not None and self.total > self.limit:
            return None
        fn = (meth, kw)
        reads = [self.alias.get(r, r) for r in reads]
        writes = [self.alias.get(w, w) for w in writes]
        writes = writes + [r for r in reads if r in self.exclusive and r not in writes]
        reads = [r for r in reads if r not in self.exclusive]
        waits = {}
        for r in reads:
            self._need(eng, self.last_w.get(r), waits)
        for w in writes:
            self._need(eng, self.last_w.get(w), waits)
            for ev in self.readers.get(w, ()):
                self._need(eng, ev, waits)
        if dma:
            s = "d%d" % self.dma_rr
            self.dma_rr = (self.dma_rr + 1) % N_DMA_SEMS
            if self.cnt[s] > 0:
                self._need(eng, (s, self.cnt[s]), waits)
            self.cnt[s] += 16
            ev = (s, self.cnt[s])
            inc = 16
        else:
            s = "e_" + eng
            self.cnt[s] += 1
            ev = (s, self.cnt[s])
            inc = 1
            if eng == 'pe':
                self.pe_plain[self.cnt[s]] = bool(skip_pe)
        for s_, v_ in waits.items():
            self.seen[eng][s_] = v_
        self.nwaits += len(waits)
        self.ops[eng].append((fn, list(waits.items()), s, inc))
        for w in writes:
            self.last_w[w] = ev
            self.readers[w] = []
        for r in reads:
            self.readers.setdefault(r, []).append(ev)
        return ev

    def emit(self, final_waits_engine="sp"):
        nc = self.nc
        fin = []
        for s, c in self.cnt.items():
            if c > 0:
                fin.append((s, c))
        with nc.Block() as block:
            def mk(engname):
                def body(e):
                    for fn, waits, s, inc in self.ops[engname]:
                        for ws, wv in waits:
                            e.wait_ge(self.sems[ws], wv)
                        ins = getattr(e, fn[0])(**fn[1])
                        ins.then_inc(self.sems[s], inc)
                    if engname == final_waits_engine:
                        for ws, wv in fin:
                            e.wait_ge(self.sems[ws], wv)
                return body
            block.tensor(mk("pe"))
            block.scalar(mk("act"))
            block.vector(mk("dve"))
            block.gpsimd(mk("pool"))
            block.sync(mk("sp"))


D = 1024
CDEC = math.exp(-0.5)
NCOL = 64 * 5 + 128 * 4
C_R, C_K, C_V, C_WL, C_AL, C_GL, C_DQ, C_DK, C_DV = 0, 64, 128, 192, 256, 320, 448, 576, 704


def build_A(S_tok=16384, BT=512, stop='all'):
    nc = bass.Bass("TRN2", target_bir_lowering=False)
    NTILE = S_tok // 128
    NB = S_tok // BT
    TPB = BT // 128
    dt_in = lambda name, shape: nc.dram_tensor(name, shape, F32, kind="ExternalInput").ap()
    xT = dt_in("xT", [D, S_tok])
    wA = dt_in("wA", [D, NCOL])
    g1d = dt_in("g1c", [128, 8])
    pv64 = dt_in("pv64", [64, 16])
    pv128 = dt_in("pv128", [128, 4])
    wdu_d = dt_in("wdu", [64, 64]); wiu_d = dt_in("wiu", [64, 64]); wgu_d = dt_in("wgu", [128, 64])
    bc64 = dt_in("bc64", [128, 3, 64])
    lamd = dt_in("lamv", [128, 4, 64])
    masks_d = dt_in("masks", [128, 5, 128])
    maskm_d = dt_in("maskm", [128, 8, 512])
    qaug_d = dt_in("qaug", [1, 512])
    abias_d = dt_in("abias", [128, 136])
    psel_d = dt_in("psel", [128, 2])
    rl_d = nc.dram_tensor("rl_d", [2, 512], F32).ap()
    y_out = nc.dram_tensor("y_rwkv", [S_tok, 64], F32, kind="ExternalOutput").ap()
    o_out = nc.dram_tensor("oT_diff", [128, S_tok // 2], F32, kind="ExternalOutput").ap()

    with ExitStack() as es:
        S = Sched(nc, es)
        S.exclusive = {"pj0", "pMa", "pO0", "pO1", "pSa", "m1", "m2", "m3"}
        S.alias = {"pMb": "m1", "pZ": "m1", "pX": "m1", "pPQ": "m2", "pG": "m2", "pY": "m2", "pY2": "m2", "pT": "m3", "pSt": "pO0", "pgt": "pO0", "prk": "pO0", "pSa0": "pSa", "pSa1": "pSa", "rb0": "tn0", "rb1": "tn1"}
        sb = lambda name, shape, dt: es.enter_context(nc.sbuf_tensor(name, shape, dt))
        ps = lambda name, shape, dt: es.enter_context(nc.psum_tensor(name, shape, dt))
        wAb = sb("wAb", [128, 8, NCOL], BF16)
        stg0_ = sb("stgA0", [128, 8, BT], F32)
        stg = [stg0_, stg0_]
        xn = sb("xn", [128, 8, BT], BF16)
        sq = xn
        rstd_b = sb("rstd_b", [128, BT], F32)
        g1c = sb("g1c_s", [128, 8], F32)
        p64 = sb("p64", [64, 16], F32); p128 = sb("p128", [128, 4], F32)
        omm64 = sb("omm64", [64, 8], F32)
        omm128 = sb("omm128", [128, 2], F32)
        wdu = sb("wdu_s", [64, 64], BF16); wiu = sb("wiu_s", [64, 64], BF16); wgu = sb("wgu_s", [128, 64], BF16)
        wtmp = sb("wtmp", [128, 64], F32)
        b64 = sb("b64", [128, 3, 64], F32); slc = sb("slc", [128, 1], F32)
        lamt = sb("lamt", [128, 4, 64], F32); lams = sb("lams", [128, 8], F32); lamb = sb("lamb", [128, 2], F32)
        masks = sb("masks_s", [128, 5, 128], F32); maskm = sb("maskm_s", [128, 8, 512], BF16)
        rls = sb("rls", [2, 512], F32); qaug_f = rls[0:1, :]; esel = sb("esel", [128, 2, 128], BF16)
        abias = sb("abias_s", [128, 136], F32); psel = sb("psel_s", [128, 2], F32)
        ones_b = sb("ones_b", [128, 128], BF16); ident_b = sb("ident_b", [128, 128], BF16); ident_f = sb("ident_f", [64, 64], F32)
        rmask = sb("rmask", [64, BT], F32)
        eps_t = sb("eps_t", [128, 4], F32)
        praw = sb("praw", [64, 5, BT + 1], F32); graw = sb("graw", [128, BT + 1], F32)
        xs = sb("xs", [64, 3, BT], F32)
        gls = sb("gls", [128, BT], BF16)
        tmp64 = sb("tmp64", [64, 6, BT], F32)
        tmp64b = sb("tmp64b", [64, 3, BT], F32)
        bf64 = sb("bf64", [64, 3, BT], BF16)
        fmops = sb("fmops", [64, TPB, 5, 128], BF16)
        gL = sb("gL", [64, BT // 64 + 1], F32)
        KT = sb("KT", [128, S_tok], BF16)
        Vtm = sb("Vtm", [128, NTILE, 128], BF16)
        Qf = sb("Qf", [128, BT], F32)
        Qs = [sb("Qs%d" % i, [128, 2, 512], BF16) for i in range(2)]; qtmp = sb("qtmp", [128, 2, 128], F32)
        qaug_pad = sb("qaug_pad", [128, 512], BF16)
        dvb = sb("dvb", [128, BT], BF16)
        Msb = sb("Msb", [128, 5, 128], BF16)
        tmops = sb("tmops", [128, 4, 64], BF16)
        Zf = sb("Zf", [128, 128], F32); Zb = sb("Zb", [128, 128], BF16)
        XY = [sb("XY%d" % i, [128, 2, 128], BF16) for i in range(2)]
        Lc = sb("Lc", [64, 2, 64], BF16); Qc = sb("Qc", [64, 2, 64], F32)
        Gz = sb("Gz", [64, 2, 128], BF16)
        Yl = sb("Yl", [128, 64], F32)
        Rst = [sb("Rst%d" % i, [64, 64], BF16) for i in range(3)]
        gtm = sb("gtm", [128, 64], F32); rkk = sb("rkk", [128, 1], F32)
        yb = sb("yb", [128, 2, 64], F32); stt = sb("stt", [128, 16], F32); bns = sb("bns", [128, 6], F32)
        Pt = sb("Pt", [128, 2, 2, 512], BF16)
        tn = xn[:, 0:4, :].rearrange("p a b -> p (a b)").bitcast(F32).rearrange("p (a b) -> p a b", b=512)
        oAB = xn[:, 4:8, :].rearrange("p a b -> p (a b)").bitcast(F32).rearrange("p (a b) -> p a b", b=512)
        bar = sb("bar", [128, 4], F32)
        Lacc = [rstd_b[:, :], Qf[:, :]]
        rb = tn
        sqo = Pt[:, 0, 0, :]
        Lhl = Pt
        pj0_t = ps("pj0", [128, 512], F32)
        S0_t = ps("S0", [128, 1024], F32); S1_t = ps("S1", [128, 1024], F32)
        pMa = S0_t[:, 0:512].rearrange("p (a b) -> p a b", b=128)
        pO0_t = ps("pO0", [128, 512], F32); pO1_t = ps("pO1", [128, 512], F32)
        pO = [pO0_t, pO1_t]
        pSa_t = S0_t[:, 512:1024]
        m1 = S1_t[:, 0:512]; m2 = S1_t[:, 512:1024]; m3f = ps("m3", [128, 512], F32); m3 = m3f[:, :].bitcast(BF16)
        pj = [m3f, m3f]
        pMb = m1[:, 0:128]; pZ = m1[:, 128:256]; pX = m1[:, 256:512].rearrange("p (a b) -> p a b", b=128)
        pPQ = m2[0:64, 0:256].rearrange("p (a b) -> p a b", b=64); pG = m2[0:64, 256:384]; pY = m2[:, 384:448]; pY2 = m2[:, 384:448]
        pT = pO0_t[:, 0:128].bitcast(BF16).rearrange("p (a b) -> p a b", b=64)
        pSt = pO0_t[0:64, 256:320]; pgt = pO0_t[:, 384:448]; prk = m2[:, 448:452]

        FM = [fmops, sb("fmops_b", [64, TPB, 5, 128], BF16)]
        GL = [gL, sb("gL_b", [64, BT // 64 + 1], F32)]
        GLS = [gls, sb("gls_b", [128, BT], BF16)]
        DVB = [dvb, sb("dvb_b", [128, BT], BF16)]
        RK = [sb("rk_a", [64, BT], BF16), sb("rk_b", [64, BT], BF16)]
        LOCALKEYS = {"Msb", "Msb4", "tmops", "Zf", "Zb", "XY0", "XY1", "Lc", "Qc", "Gz", "Yl", "gtm", "rkk", "y0", "y1", "stt0", "stt1", "stt2", "stt3",
                     "pMa", "pMb", "pZ", "pX", "pT", "pPQ", "pG", "pY", "pY2", "pSt", "pgt", "prk"}
        Msb_b = sb("Msb_b", [128, 5, 128], BF16); tmops_b = sb("tmops_b", [128, 4, 64], BF16)
        Zf_b = sb("Zf_b", [128, 128], F32); Zb_b = sb("Zb_b", [128, 128], BF16)
        XY_b = [sb("XYb%d" % i, [128, 2, 128], BF16) for i in range(2)]
        Lc_b = sb("Lc_b", [64, 2, 64], BF16); Qc_b = sb("Qc_b", [64, 2, 64], F32); Gz_b = sb("Gz_b", [64, 2, 128], BF16)
        Yl_b = sb("Yl_b", [128, 64], F32); gtm_b = sb("gtm_b", [128, 64], F32); rkk_b = sb("rkk_b", [128, 1], F32)
        yb_b = sb("yb_b", [128, 2, 64], F32); stt_b = sb("stt_b", [128, 16], F32)
        TB = [(Msb, tmops, Zf, Zb, XY, Lc, Qc, Gz, Yl, gtm, rkk, yb, stt),
              (Msb_b, tmops_b, Zf_b, Zb_b, XY_b, Lc_b, Qc_b, Gz_b, Yl_b, gtm_b, rkk_b, yb_b, stt_b)]
        TP = [(pMa, pMb, pZ, pX, pT, pPQ, pG, pY, pY2, pSt, pgt, prk),
              (pSa_t[:, :].rearrange("p (a b) -> p a b", b=128), pO1_t[:, 0:128], pO1_t[:, 128:256], pO1_t[:, 256:512].rearrange("p (a b) -> p a b", b=128),
               pO0_t[:, 128:256].bitcast(BF16).rearrange("p (a b) -> p a b", b=64),
               pj0_t[0:64, 0:256].rearrange("p (a b) -> p a b", b=64), pj0_t[0:64, 256:384], pj0_t[:, 384:448], pj0_t[:, 384:448],
               pO0_t[0:64, 320:384], pO0_t[:, 448:512], pj0_t[:, 448:452])]
        for k_, v_ in (("pMa", "pMa"), ("pMb", "m1"), ("pZ", "m1"), ("pX", "m1"), ("pPQ", "m2"), ("pG", "m2"), ("pY", "m2"), ("pY2", "m2"), ("pT", "pO0"), ("pSt", "pO0"), ("pgt", "pO0"), ("prk", "m2")):
            S.alias[k_ + "@0"] = v_
        for k_, v_ in (("pMa", "pSa"), ("pMb", "pO1"), ("pZ", "pO1"), ("pX", "pO1"), ("pPQ", "pj0"), ("pG", "pj0"), ("pY", "pj0"), ("pY2", "pj0"), ("pT", "pO0"), ("pSt", "pO0"), ("pgt", "pO0"), ("prk", "pj0")):
            S.alias[k_ + "@1"] = v_

        def dma(out, in_, w, r=()):
            S.op("sp", "dma_start", dict(out=out, in_=in_), reads=list(r), writes=list(w), dma=True)

        for dst, src, k in ((g1c, g1d, "g1c"), (p64, pv64, "p64"), (p128, pv128, "p128"), (b64, bc64, "b64"),
                            (lamt, lamd, "lamt"), (masks, masks_d, "masks"), (abias, abias_d, "abias"), (psel, psel_d, "psel")):
            nd = len(src.shape)
            sl = tuple(slice(None) for _ in range(nd))
            dma(dst[sl], src[sl], [k])
        dma(qaug_f[:, :], qaug_d[:, :], ["rls"])
        S.op("pool", "memset", dict(ap=qaug_pad[:, :], constant=0.0), writes=["qaug_pad"])
        S.op("dve", "tensor_copy", dict(out=qaug_pad[0:1, :], in_=qaug_f[:, :]), reads=["rls"], writes=["qaug_pad"])
        for mp_ in range(2):
            S.op("pool", "memset", dict(ap=Qs[mp_][:, :, :], constant=0.0), writes=["Qs%d" % mp_])
        for jr_ in range(8):
            dma(stg0_[:, 0, :], maskm_d[:, jr_, :], ["stg0"])
            S.op("dve", "tensor_copy", dict(out=maskm[:, jr_, :], in_=stg0_[:, 0, :]), reads=["stg0"], writes=["maskm"])
        S.op("pool", "memset", dict(ap=esel[:, :, :], constant=0.0), writes=["esel"])
        S.op("pool", "memset", dict(ap=esel[:, 0, 0:1], constant=1.0), writes=["esel"])
        S.op("pool", "memset", dict(ap=esel[:, 1, 1:2], constant=1.0), writes=["esel"])
        for dstw, srcw, rows in ((wdu, wdu_d, 64), (wiu, wiu_d, 64), (wgu, wgu_d, 128)):
            dma(wtmp[0:rows, :], srcw[:, :], ["wtmp"])
            S.op("dve", "tensor_copy", dict(out=dstw[:, :], in_=wtmp[0:rows, :]), reads=["wtmp"], writes=["wsm"])
        S.op("pool", "memset", dict(ap=ones_b[:, :], constant=1.0), writes=["ones_b"])
        S.op("pool", "memset", dict(ap=ident_b[:, :], constant=1.0), writes=["ident_b"])
        S.op("pool", "affine_select", dict(out=ident_b[:, :], in_=ident_b[:, :], pattern=[[-1, 128]], compare_op=ALU.is_equal, fill=0.0, base=0, channel_multiplier=1),
             reads=["ident_b"], writes=["ident_b"])
        S.op("pool", "memset", dict(ap=ident_f[:, :], constant=1.0), writes=["ident_f"])
        S.op("pool", "affine_select", dict(out=ident_f[:, :], in_=ident_f[:, :], pattern=[[-1, 64]], compare_op=ALU.is_equal, fill=0.0, base=0, channel_multiplier=1),
             reads=["ident_f"], writes=["ident_f"])
        S.op("pool", "memset", dict(ap=rmask[:, :], constant=1.0), writes=["rmask"])
        S.op("pool", "memset", dict(ap=rmask[:, :].rearrange("p (c l) -> p c l", l=64)[:, :, 0:1], constant=0.0), writes=["rmask"])
        S.op("pool", "memset", dict(ap=eps_t[:, 0:1], constant=1e-6), writes=["eps"])
        S.op("pool", "memset", dict(ap=eps_t[:, 1:2], constant=64e-5), writes=["eps"])
        S.op("pool", "memset", dict(ap=eps_t[:, 2:3], constant=1e-5), writes=["eps"])
        S.op("pool", "memset", dict(ap=eps_t[:, 3:4], constant=1e-24), writes=["eps"])
        S.op("pool", "memset", dict(ap=praw[:, :, 0:1], constant=0.0), writes=["praw%d" % i for i in range(5)])
        S.op("pool", "memset", dict(ap=graw[:, 0:1], constant=0.0), writes=["graw"])
        S.op("pool", "memset", dict(ap=Gz[:, :, :], constant=0.0), writes=["Gz@0"])
        S.op("pool", "memset", dict(ap=Gz_b[:, :, :], constant=0.0), writes=["Gz@1"])
        S.op("pool", "memset", dict(ap=Rst[0][:, :], constant=0.0), writes=["R0"])
        S.op("pool", "memset", dict(ap=gL[:, 0:1], constant=1.0), writes=["gL_0"])
        S.op("dve", "tensor_scalar", dict(out=omm64[:, 0:5], in0=p64[:, 0:5], scalar1=-1.0, scalar2=1.0, op0=ALU.mult, op1=ALU.add), reads=["p64"], writes=["omm64"])
        S.op("dve", "tensor_scalar", dict(out=omm64[:, 5:6], in0=p64[:, 8:9], scalar1=-1.0, scalar2=1.0, op0=ALU.mult, op1=ALU.add), reads=["p64"], writes=["omm64"])
        S.op("dve", "tensor_scalar", dict(out=omm128[:, 0:1], in0=p128[:, 0:1], scalar1=-1.0, scalar2=1.0, op0=ALU.mult, op1=ALU.add), reads=["p128"], writes=["omm128"])
        S.op("dve", "tensor_scalar", dict(out=slc[:, :], in0=p128[:, 1:2], scalar1=0.8, scalar2=None, op0=ALU.mult), reads=["p128"], writes=["slc"])
        S.op("dve", "tensor_tensor", dict(out=lamt[:, 0, :], in0=lamt[:, 0, :], in1=lamt[:, 1, :], op=ALU.mult), reads=["lamt"], writes=["lamt"])
        S.op("dve", "tensor_tensor", dict(out=lamt[:, 2, :], in0=lamt[:, 2, :], in1=lamt[:, 3, :], op=ALU.mult), reads=["lamt"], writes=["lamt"])
        S.op("dve", "reduce_sum", dict(out=lams[:, 0:1], in_=lamt[:, 0, :], axis=AX.X), reads=["lamt"], writes=["lams"])
        S.op("dve", "reduce_sum", dict(out=lams[:, 1:2], in_=lamt[:, 2, :], axis=AX.X), reads=["lamt"], writes=["lams"])
        S.op("act", "activation", dict(out=lams[:, 2:4], in_=lams[:, 0:2], func=AF.Exp), reads=["lams"], writes=["lams"])
        S.op("dve", "tensor_tensor", dict(out=lams[:, 4:5], in0=lams[:, 3:4], in1=lams[:, 2:3], op=ALU.subtract), reads=["lams"], writes=["lams"])
        S.op("dve", "tensor_scalar", dict(out=lams[:, 5:6], in0=lams[:, 4:5], scalar1=-0.2, scalar2=None, op0=ALU.add), reads=["lams"], writes=["lams"])
        S.op("dve", "tensor_copy", dict(out=lamb[:, 0:1], in_=lams[:, 5:6]), reads=["lams"], writes=["lamb"])
        for k in range(8):
            st_ = stg[k % 2]
            src = st_[:, 0:2, :].rearrange("p a b -> p (a b)")[:, 0:NCOL]
            dma(src, wA[k * 128:(k + 1) * 128, :], ["stg0"])
            if k % 2:
                S.op("dve", "tensor_copy", dict(out=wAb[:, k, :], in_=src), reads=["stg0"], writes=["wAb"])
            else:
                S.op("act", "activation", dict(out=wAb[:, k, :], in_=src, func=AF.Copy), reads=["stg0"], writes=["wAb"])
        rkb = sb("rkb", [64, 2], BF16)
        S.op("dve", "tensor_copy", dict(out=rkb[:, 0:1], in_=p64[:, 9:10]), reads=["p64"], writes=["rkb"])
        S.op("dve", "tensor_copy", dict(out=rkb[:, 1:2], in_=p64[:, 9:10]), reads=["p64"], writes=["rkb"])

        groups = [("r", C_R, 64), ("k", C_K, 64), ("v", C_V, 64), ("wl", C_WL, 64), ("al", C_AL, 64), ("gl", C_GL, 128),
                  ("dq", C_DQ, 128), ("dk", C_DK, 128), ("dv", C_DV, 128)]
        pj_i = [0]
        rcur = [0]
        chunk_ctr = [0]

        def build_pre(bi):
            ops_ = []
            def OPp(eng, meth, kw, reads=(), writes=(), dma=False, skip_pe=False):
                ops_.append((eng, meth, kw, list(reads), list(writes), dma, skip_pe))
            def DMAp(out, in_, w, r=()):
                OPp("sp", "dma_start", dict(out=out, in_=in_), reads=list(r), writes=list(w), dma=True)
            if bi > 0:
                OPp("pool", "tensor_copy", dict(out=GL[bi % 2][:, 0:1], in_=GL[(bi - 1) % 2][:, BT // 64:BT // 64 + 1]), reads=["gLn_%d" % ((bi - 1) % 2)], writes=["gL_%d" % (bi % 2)])
            t0 = bi * BT
            t0 = bi * BT
            sg_ = stg[bi % 2]; sk = "stg0"
            if bi == 0:
                DMAp(sg_[:, :, :], xT[:, t0:t0 + BT].rearrange("(k p) t -> p k t", p=128), [sk])
            OPp("act", "activation", dict(out=sq[:, :, :], in_=sg_[:, :, :], func=AF.Square), reads=[sk], writes=["xn"])
            pn = pj[pj_i[0] % 2]; pnk = "m3"; pj_i[0] += 1
            for k in range(8):
                OPp("pe", "matmul", dict(out=pn[:, 0:BT], lhsT=ones_b[:, :], rhs=sq[:, k, :], start=(k == 0), stop=(k == 7)), reads=["xn", "ones_b"], writes=[pnk], skip_pe=(k > 0))
            OPp("act", "activation", dict(out=rstd_b[:, :], in_=pn[:, 0:BT], func=AF.Sqrt, scale=1.0 / D, bias=eps_t[:, 0:1]), reads=[pnk, "eps"], writes=["rstd_b"])
            OPp("dve", "reciprocal", dict(out=rstd_b[:, :], in_=rstd_b[:, :]), reads=["rstd_b"], writes=["rstd_b"])
            for k in range(8):
                OPp("dve", "scalar_tensor_tensor", dict(out=xn[:, k, :], in0=sg_[:, k, :], scalar=g1c[:, k:k + 1], in1=rstd_b[:, :], op0=ALU.mult, op1=ALU.mult),
                     reads=[sk, "g1c", "rstd_b"], writes=["xn"])
            if bi + 1 < NB:
                DMAp(sg_[:, :, :], xT[:, t0 + BT:t0 + 2 * BT].rearrange("(k p) t -> p k t", p=128), [sk])
            for gi, (gname, c0, M) in enumerate(groups):
                pp = pj[pj_i[0] % 2]; ppk = "m3"; pj_i[0] += 1
                for k in range(8):
                    OPp("pe", "matmul", dict(out=pp[0:M, 0:BT], lhsT=wAb[:, k, c0:c0 + M], rhs=xn[:, k, :], start=(k == 0), stop=(k == 7)), reads=["wAb", "xn"], writes=[ppk], skip_pe=(k > 0))
                if gi < 5:
                    OPp("act", "activation", dict(out=praw[:, gi, 1:BT + 1], in_=pp[0:64, 0:BT], func=AF.Identity), reads=[ppk], writes=["praw%d" % gi])
                    OPp("act", "activation", dict(out=tmp64[:, 0, :], in_=praw[:, gi, 0:BT], func=AF.Identity, scale=p64[:, gi:gi + 1]),
                         reads=["praw%d" % gi, "p64"], writes=["t0"])
                    OPp("dve", "scalar_tensor_tensor", dict(out=(xs[:, gi, :] if gi < 3 else tmp64b[:, gi - 2, :]), in0=praw[:, gi, 1:BT + 1], scalar=omm64[:, gi:gi + 1], in1=tmp64[:, 0, :], op0=ALU.mult, op1=ALU.add),
                         reads=["praw%d" % gi, "omm64", "t0"], writes=[("xs%d" % gi) if gi < 3 else ("kp", "bb")[gi - 3]])
                    OPp("pool", "tensor_copy", dict(out=praw[:, gi, 0:1], in_=praw[:, gi, BT:BT + 1]), reads=["praw%d" % gi], writes=["praw%d" % gi])
                elif gname == "gl":
                    OPp("act", "activation", dict(out=graw[:, 1:BT + 1], in_=pp[:, 0:BT], func=AF.Identity), reads=[ppk], writes=["graw"])
                    OPp("act", "activation", dict(out=Qf[:, :], in_=graw[:, 0:BT], func=AF.Identity, scale=p128[:, 0:1]), reads=["graw", "p128"], writes=["Qf"])
                    OPp("dve", "scalar_tensor_tensor", dict(out=Qf[:, :], in0=graw[:, 1:BT + 1], scalar=omm128[:, 0:1], in1=Qf[:, :], op0=ALU.mult, op1=ALU.add),
                         reads=["graw", "omm128", "Qf"], writes=["Qf"])
                    OPp("act", "activation", dict(out=GLS[bi % 2][:, :], in_=Qf[:, :], func=AF.Sigmoid), reads=["Qf"], writes=["gls_%d" % (bi % 2)])
                    OPp("pool", "tensor_copy", dict(out=graw[:, 0:1], in_=graw[:, BT:BT + 1]), reads=["graw"], writes=["graw"])
                elif gname == "dq":
                    OPp("act", "activation", dict(out=Qf[:, :], in_=pp[:, 0:BT], func=AF.Identity), reads=[ppk, "gls_%d" % (bi % 2)], writes=["Qf"])
                    qv = Qf[:, :].rearrange("p (a b l) -> p a b l", b=2, l=128)
                    gq = (bi // 2) % 2; i0 = 2 * (bi % 2)
                    OPp("dve", "tensor_scalar", dict(out=qtmp[:, :, :], in0=qv[:, :, 0, :], scalar1=psel[:, 0:1], scalar2=None, op0=ALU.mult), reads=["Qf", "psel"], writes=["qtmp"])
                    for mq in range(2):
                        rs_ = slice(64 * mq, 64 * mq + 64)
                        OPp("dve", "scalar_tensor_tensor", dict(out=Qs[mq][rs_, gq, i0 * 128:(i0 + 2) * 128].rearrange("p (a l) -> p a l", l=128), in0=qv[rs_, :, 1, :], scalar=psel[rs_, 1:2], in1=qtmp[rs_, :, :], op0=ALU.mult, op1=ALU.add),
                             reads=["Qf", "psel", "qtmp"], writes=["Qs%d" % mq])
                elif gname == "dk":
                    OPp("act", "activation", dict(out=KT[:, t0:t0 + BT], in_=pp[:, 0:BT], func=AF.Copy), reads=[ppk], writes=["KT"])
                else:
                    OPp("act", "activation", dict(out=DVB[bi % 2][:, :], in_=pp[:, 0:BT], func=AF.Copy), reads=[ppk], writes=["dvb_%d" % (bi % 2)])
            r_, k_, v_ = (xs[:, i, :] for i in range(3)); wl_ = tmp64b[:, 1, :]; al_ = tmp64b[:, 2, :]
            OPp("act", "activation", dict(out=bf64[:, 0, :], in_=wl_, func=AF.Tanh), reads=["kp"], writes=["bf0"])
            OPp("pool", "tensor_copy", dict(out=bf64[:, 1, :], in_=al_), reads=["bb"], writes=["bf1"])
            pw = pj[pj_i[0] % 2]; pwk = "m3"; pj_i[0] += 1
            OPp("pe", "matmul", dict(out=pw[0:64, 0:BT], lhsT=wdu[:, :], rhs=bf64[:, 0, :], start=True, stop=True), reads=["wsm", "bf0"], writes=[pwk])
            OPp("act", "activation", dict(out=tmp64[:, 2, :], in_=pw[0:64, 0:BT], func=AF.Sigmoid, bias=p64[:, 5:6]), reads=[pwk, "p64"], writes=["sg"])
            pa_ = pj[pj_i[0] % 2]; pak = "m3"; pj_i[0] += 1
            OPp("pe", "matmul", dict(out=pa_[0:64, 0:BT], lhsT=wiu[:, :], rhs=bf64[:, 1, :], start=True, stop=True), reads=["wsm", "bf1"], writes=[pak])
            OPp("act", "activation", dict(out=tmp64[:, 1, :], in_=pa_[0:64, 0:BT], func=AF.Sigmoid, bias=p64[:, 6:7]), reads=[pak, "p64"], writes=["a"])
            OPp("dve", "tensor_scalar", dict(out=tmp64[:, 3, :], in0=k_, scalar1=p64[:, 7:8], scalar2=None, op0=ALU.mult), reads=["xs1", "p64"], writes=["kk"])
            OPp("act", "activation", dict(out=bf64[:, 2, :], in_=tmp64[:, 3, :], func=AF.Square), reads=["kk"], writes=["bf2"])
            pk_ = pj[pj_i[0] % 2]; pkk = "m3"; pj_i[0] += 1
            OPp("pe", "matmul", dict(out=pk_[0:64, 0:BT], lhsT=ones_b[0:64, 0:64], rhs=bf64[:, 2, :], start=True, stop=True), reads=["ones_b", "bf2"], writes=[pkk])
            OPp("act", "activation", dict(out=tmp64[:, 0, :], in_=pk_[0:64, 0:BT], func=AF.Sqrt, bias=eps_t[0:64, 3:4]), reads=[pkk, "eps"], writes=["t0"])
            OPp("dve", "reciprocal", dict(out=tmp64[:, 0, :], in_=tmp64[:, 0, :]), reads=["t0"], writes=["t0"])
            OPp("dve", "tensor_tensor", dict(out=tmp64[:, 3, :], in0=tmp64[:, 3, :], in1=tmp64[:, 0, :], op=ALU.mult), reads=["kk", "t0"], writes=["kk"])
            OPp("dve", "tensor_scalar", dict(out=tmp64b[:, 1, :], in0=tmp64[:, 1, :], scalar1=p64[:, 8:9], scalar2=omm64[:, 5:6], op0=ALU.mult, op1=ALU.add),
                 reads=["a", "p64", "omm64"], writes=["kp"])
            OPp("pool", "tensor_tensor", dict(out=tmp64b[:, 1, :], in0=tmp64b[:, 1, :], in1=k_, op=ALU.mult), reads=["kp", "xs1"], writes=["kp"])
            OPp("pool", "tensor_tensor", dict(out=tmp64b[:, 2, :], in0=tmp64[:, 3, :], in1=tmp64[:, 1, :], op=ALU.mult), reads=["kk", "a"], writes=["bb"])
            OPp("dve", "tensor_tensor_scan", dict(out=tmp64[:, 0, :], data0=rmask[:, :], data1=tmp64[:, 2, :], initial=0.0, op0=ALU.mult, op1=ALU.add),
                 reads=["rmask", "sg"], writes=["t0"])
            OPp("act", "activation", dict(out=tmp64[:, 4, :], in_=tmp64[:, 0, :], func=AF.Exp, scale=CDEC), reads=["t0"], writes=["epos"])
            OPp("act", "activation", dict(out=tmp64[:, 5, :], in_=tmp64[:, 0, :], func=AF.Exp, scale=-CDEC), reads=["t0"], writes=["eneg"])
            OPp("pool", "tensor_tensor", dict(out=tmp64[:, 0, :], in0=tmp64[:, 0, :], in1=tmp64[:, 2, :], op=ALU.subtract), reads=["t0", "sg"], writes=["t0"])
            OPp("act", "activation", dict(out=tmp64b[:, 0, :], in_=tmp64[:, 0, :], func=AF.Exp, scale=-CDEC), reads=["t0"], writes=["eprev"])
            OPp("pool", "tensor_copy", dict(out=GL[bi % 2][:, 1:BT // 64 + 1], in_=tmp64[:, 5, :].rearrange("p (c l) -> p c l", l=64)[:, :, 63]), reads=["eneg"], writes=["gLn_%d" % (bi % 2)])
            fm = lambda q: FM[bi % 2][:, :, q, :]
            v3 = lambda ap: ap.rearrange("p (t l) -> p t l", l=128)
            OPp("dve", "scalar_tensor_tensor", dict(out=fm(0), in0=v3(tmp64[:, 3, :]), scalar=-1.0, in1=v3(tmp64b[:, 0, :]), op0=ALU.mult, op1=ALU.mult),
                 reads=["kk", "eprev"], writes=["fm0_%d" % (bi % 2)])
            OPp("pool", "tensor_tensor", dict(out=fm(1), in0=v3(r_), in1=v3(tmp64[:, 5, :]), op=ALU.mult), reads=["xs0", "eneg"], writes=["fm1_%d" % (bi % 2)])
            OPp("dve", "tensor_tensor", dict(out=fm(2), in0=v3(tmp64b[:, 2, :]), in1=v3(tmp64[:, 4, :]), op=ALU.mult), reads=["bb", "epos"], writes=["fm2_%d" % (bi % 2)])
            OPp("pool", "tensor_tensor", dict(out=fm(3), in0=v3(tmp64b[:, 1, :]), in1=v3(tmp64[:, 4, :]), op=ALU.mult), reads=["kp", "epos"], writes=["fm3_%d" % (bi % 2)])
            OPp("act", "activation", dict(out=fm(4), in_=v3(v_), func=AF.Copy), reads=["xs2"], writes=["fm4_%d" % (bi % 2)])
            OPp("dve", "tensor_tensor", dict(out=RK[bi % 2][:, :], in0=r_, in1=tmp64b[:, 1, :], op=ALU.mult), reads=["xs0", "kp"], writes=["rk_%d" % (bi % 2)])

            return ops_

        pre0 = build_pre(0)
        for o2 in pre0:
            S.op(o2[0], o2[1], o2[2], reads=o2[3], writes=o2[4], dma=o2[5], skip_pe=o2[6])
        for bi in range(NB):
            t0 = bi * BT
            nxt_pre = build_pre(bi + 1) if bi + 1 < NB else []
            if stop == 'pre':
                for o2 in nxt_pre:
                    S.op(o2[0], o2[1], o2[2], reads=o2[3], writes=o2[4], dma=o2[5], skip_pe=o2[6])
            def tile_block(ti, th):
                pre_, st_ = [], []
                cur = [pre_]
                def km(k):
                    return (k + "@%d" % th) if k in LOCALKEYS else k
                def OP(eng, meth, kw, reads=(), writes=(), dma=False, skip_pe=False):
                    cur[0].append((eng, meth, kw, [km(r) for r in reads], [km(w) for w in writes], dma, skip_pe))
                def DMA(out, in_, w, r=()):
                    OP("sp", "dma_start", dict(out=out, in_=in_), reads=list(r), writes=list(w), dma=True)
                Msb, tmops, Zf, Zb, XY, Lc, Qc, Gz, Yl, gtm, rkk, yb, stt = TB[th]
                pMa, pMb, pZ, pX, pT, pPQ, pG, pY, pY2, pSt, pgt, prk = TP[th]
                tg = bi * TPB + ti
                tl = slice(ti * 128, (ti + 1) * 128)
                A_T, R_T, B_T, K_T, V_T = (FM[bi % 2][:, ti, q, :] for q in range(5))
                AR = FM[bi % 2][:, ti, 0:2, :].rearrange("p a b -> p (a b)")
                OP("pe", "transpose", dict(out=pT[:, 0:2, :].rearrange("p a b -> p (a b)"), in_=DVB[bi % 2][:, tl], identity=ident_b[:, :]), reads=["dvb_%d" % (bi % 2), "ident_b"], writes=["pT"])
                OP("act", "activation", dict(out=Vtm[:, tg, :], in_=pT[:, 0:2, :].rearrange("p a b -> p (a b)"), func=AF.Copy), reads=["pT"], writes=["Vtm"])
                OP("pe", "matmul", dict(out=pMa[:, 0:2, :].rearrange("p a b -> p (a b)"), lhsT=B_T, rhs=AR, start=True, stop=True), reads=["fm0_%d" % (bi % 2), "fm1_%d" % (bi % 2), "fm2_%d" % (bi % 2)], writes=["pMa"])
                OP("pe", "matmul", dict(out=pMa[:, 2:4, :].rearrange("p a b -> p (a b)"), lhsT=K_T, rhs=AR, start=True, stop=True), reads=["fm0_%d" % (bi % 2), "fm1_%d" % (bi % 2), "fm3_%d" % (bi % 2)], writes=["pMa"])
                OP("pe", "matmul", dict(out=pMb[:, :], lhsT=A_T, rhs=B_T, start=True, stop=True), reads=["fm0_%d" % (bi % 2), "fm2_%d" % (bi % 2)], writes=["pMb"])
                OP("dve", "tensor_tensor", dict(out=Msb[:, 0:4, :], in0=pMa[:, :, :], in1=masks[:, 0:4, :], op=ALU.mult), reads=["pMa", "masks"], writes=["Msb"])
                OP("dve", "tensor_tensor", dict(out=Msb[:, 4, :], in0=pMb[:, :], in1=masks[:, 4, :], op=ALU.mult), reads=["pMb", "masks"], writes=["Msb4"])
                for qi_, src in enumerate((A_T, B_T, K_T, V_T)):
                    OP("pe", "transpose", dict(out=pT[:, qi_, :], in_=src, identity=ident_b[0:64, 0:64]), reads=["fm0_%d" % (bi % 2), "fm2_%d" % (bi % 2), "fm3_%d" % (bi % 2), "fm4_%d" % (bi % 2), "ident_b"], writes=["pT"])
                OP("act", "activation", dict(out=tmops[:, :, :], in_=pT[:, :, :], func=AF.Copy), reads=["pT"], writes=["tmops"])
                A_tm, B_tm, K_tm, V_tm = (tmops[:, q, :] for q in range(4))
                OP("pe", "matmul", dict(out=pZ[:, 64:128], lhsT=Msb[:, 2, :], rhs=V_tm, start=True, stop=True), reads=["Msb", "tmops"], writes=["pZ"])
                OP("dve", "tensor_copy", dict(out=Zf[:, 0:64], in_=A_tm), reads=["tmops"], writes=["Zf"])
                OP("dve", "tensor_copy", dict(out=Zf[:, 64:128], in_=pZ[:, 64:128]), reads=["pZ"], writes=["Zf"])
                OP("act", "activation", dict(out=Zb[:, :], in_=Zf[:, :], func=AF.Copy), reads=["Zf"], writes=["Zb"])
                X = Msb[:, 0, :]; Yp = Msb[:, 4, :]; xk = ["Msb", "Msb4"]
                for lv in range(6):
                    OP("pe", "matmul", dict(out=pZ[:, :], lhsT=X, rhs=Zb[:, :], start=True, stop=True), reads=xk + ["Zb"], writes=["pZ"])
                    OP("dve", "tensor_tensor", dict(out=Zf[:, :], in0=Zf[:, :], in1=pZ[:, :], op=ALU.add), reads=["Zf", "pZ"], writes=["Zf"])
                    OP("act", "activation", dict(out=Zb[:, :], in_=Zf[:, :], func=AF.Copy), reads=["Zf"], writes=["Zb"])
                    if lv < 5:
                        OP("pe", "matmul", dict(out=pX[:, 0, :], lhsT=Yp, rhs=X, start=True, stop=True), reads=xk, writes=["pX"])
                        OP("pe", "matmul", dict(out=pX[:, 1, :], lhsT=X, rhs=Yp, start=True, stop=True), reads=xk, writes=["pX"])
                        nxt = XY[lv % 2]
                        OP("pool" if False else "dve", "tensor_copy", dict(out=nxt[:, :, :], in_=pX[:, :, :]), reads=["pX"], writes=["XY%d" % (lv % 2)])
                        X = nxt[:, 0, :]; Yp = nxt[:, 1, :]; xk = ["XY%d" % (lv % 2)]
                Ah = Zb[:, 0:64]; U = Zb[:, 64:128]
                for c in range(2):
                    cs_ = slice(c * 64, (c + 1) * 64)
                    OP("pe", "matmul", dict(out=pPQ[:, c, :], lhsT=Zb[cs_, 0:64], rhs=tmops[cs_, 1, :], start=True, stop=False), reads=["Zb", "tmops"], writes=["pPQ"])
                    OP("pe", "matmul", dict(out=pPQ[:, c, :], lhsT=ident_b[0:64, 0:64], rhs=ident_b[0:64, 0:64], start=False, stop=True), reads=["ident_b"], writes=["pPQ"])
                    OP("pe", "matmul", dict(out=pPQ[:, 2 + c, :], lhsT=tmops[cs_, 1, :], rhs=Zb[cs_, 64:128], start=True, stop=False), reads=["Zb", "tmops"], writes=["pPQ"])
                    OP("pe", "matmul", dict(out=pPQ[:, 2 + c, :], lhsT=tmops[cs_, 2, :], rhs=tmops[cs_, 3, :], start=False, stop=True), reads=["tmops"], writes=["pPQ"])
                gcol = lambda c: GL[bi % 2][:, 2 * ti + c:2 * ti + c + 1]
                for c in range(2):
                    OP("act", "activation", dict(out=Lc[:, c, :], in_=pPQ[:, c, :], func=AF.Identity, scale=gcol(c)), reads=["pPQ", "gL_%d" % (bi % 2), "gLn_%d" % (bi % 2)], writes=["Lc"])
                OP("dve", "tensor_copy", dict(out=Qc[:, :, :], in_=pPQ[:, 2:4, :]), reads=["pPQ"], writes=["Qc"])
                OP("pe", "matmul", dict(out=pG[:, :], lhsT=Zb[:, 0:64], rhs=Msb[:, 1, :], start=True, stop=True), reads=["Zb", "Msb"], writes=["pG"])
                for c in range(2):
                    cs_ = slice(c * 64, (c + 1) * 64)
                    OP("dve", "tensor_tensor", dict(out=Gz[:, c, cs_], in0=pG[:, cs_], in1=R_T[:, cs_], op=ALU.add), reads=["pG", "fm1_%d" % (bi % 2)], writes=["Gz"])
                    OP("dve", "tensor_scalar", dict(out=Gz[:, c, cs_], in0=Gz[:, c, cs_], scalar1=gcol(c), scalar2=None, op0=ALU.mult), reads=["Gz", "gL_%d" % (bi % 2), "gLn_%d" % (bi % 2)], writes=["Gz"])
                OP("pe", "matmul", dict(out=pY[:, :], lhsT=Msb[:, 1, :], rhs=Zb[:, 64:128], start=True, stop=False), reads=["Msb", "Zb"], writes=["pY"])
                OP("pe", "matmul", dict(out=pY[:, :], lhsT=Msb[:, 3, :], rhs=V_tm, start=False, stop=True), reads=["Msb", "tmops"], writes=["pY"])
                OP("act", "activation", dict(out=Yl[:, :], in_=pY[:, :], func=AF.Identity), reads=["pY"], writes=["Yl"])
                OP("pe", "matmul", dict(out=pgt[:, :], lhsT=GLS[bi % 2][:, tl], rhs=wgu[:, :], start=True, stop=True), reads=["gls_%d" % (bi % 2), "wsm"], writes=["pgt"])
                OP("act", "activation", dict(out=gtm[:, :], in_=pgt[:, :], func=AF.Identity), reads=["pgt"], writes=["gtm"])
                OP("pe", "matmul", dict(out=prk[:, 2:4], lhsT=RK[bi % 2][:, tl], rhs=rkb[:, :], start=True, stop=True), reads=["rk_%d" % (bi % 2), "rkb"], writes=["prk"])
                OP("dve", "tensor_copy", dict(out=rkk[:, :], in_=prk[:, 2:3]), reads=["prk"], writes=["rkk"])
                cur[0] = st_
                Ra = Rst[rcur[0] % 3]; Rb = Rst[(rcur[0] + 1) % 3]; Rc2 = Rst[(rcur[0] + 2) % 3]
                ka, kb_, kc_ = "R%d" % (rcur[0] % 3), "R%d" % ((rcur[0] + 1) % 3), "R%d" % ((rcur[0] + 2) % 3)
                OP("pe", "matmul", dict(out=pSt[:, :], lhsT=Lc[:, 0, :], rhs=Ra[:, :], start=True, stop=True), reads=["Lc", ka], writes=["pSt"])
                OP("dve", "tensor_tensor", dict(out=Rb[:, :], in0=pSt[:, :], in1=Qc[:, 0, :], op=ALU.add), reads=["pSt", "Qc"], writes=[kb_])
                OP("pe", "matmul", dict(out=pSt[:, :], lhsT=Lc[:, 1, :], rhs=Rb[:, :], start=True, stop=True), reads=["Lc", kb_], writes=["pSt"])
                OP("dve", "tensor_tensor", dict(out=Rc2[:, :], in0=pSt[:, :], in1=Qc[:, 1, :], op=ALU.add), reads=["pSt", "Qc"], writes=[kc_])
                OP("pe", "matmul", dict(out=pY2[:, :], lhsT=Gz[:, 0, :], rhs=Ra[:, :], start=True, stop=False), reads=["Gz", ka], writes=["pY2"])
                OP("pe", "matmul", dict(out=pY2[:, :], lhsT=Gz[:, 1, :], rhs=Rb[:, :], start=False, stop=True), reads=["Gz", kb_], writes=["pY2"])
                rcur[0] += 2
                y0 = yb[:, 0, :]; y1 = yb[:, 1, :]
                OP("dve", "tensor_tensor", dict(out=y0, in0=pY2[:, :], in1=Yl[:, :], op=ALU.add), reads=["pY2", "Yl"], writes=["y0"])
                OP("dve", "scalar_tensor_tensor", dict(out=y0, in0=V_tm, scalar=rkk[:, 0:1], in1=y0, op0=ALU.mult, op1=ALU.add), reads=["tmops", "rkk", "y0"], writes=["y0"])
                OP("act", "activation", dict(out=y1, in_=y0, func=AF.Identity, accum_out=stt[:, 0:1]), reads=["y0"], writes=["y1", "stt0"])
                OP("dve", "tensor_scalar", dict(out=stt[:, 1:2], in0=stt[:, 0:1], scalar1=-1.0 / 64, scalar2=None, op0=ALU.mult), reads=["stt0"], writes=["stt1"])
                OP("act", "activation", dict(out=y1, in_=y0, func=AF.Square, bias=stt[:, 1:2], accum_out=stt[:, 2:3]), reads=["y0", "stt1"], writes=["y1", "stt2"])
                OP("act", "activation", dict(out=stt[:, 3:4], in_=stt[:, 2:3], func=AF.Sqrt, scale=1.0 / 64, bias=eps_t[:, 1:2]), reads=["stt2", "eps"], writes=["stt3"])
                OP("dve", "reciprocal", dict(out=stt[:, 3:4], in_=stt[:, 3:4]), reads=["stt3"], writes=["stt3"])
                OP("dve", "tensor_scalar", dict(out=y1, in0=y0, scalar1=stt[:, 1:2], scalar2=stt[:, 3:4], op0=ALU.add, op1=ALU.mult), reads=["y0", "stt1", "stt3"], writes=["y1"])
                OP("pool", "tensor_tensor", dict(out=y1, in0=y1, in1=b64[:, 0, :], op=ALU.mult), reads=["y1", "b64"], writes=["y1"])
                OP("pool", "tensor_tensor", dict(out=y1, in0=y1, in1=b64[:, 1, :], op=ALU.add), reads=["y1", "b64"], writes=["y1"])
                OP("pool", "tensor_tensor", dict(out=y1, in0=y1, in1=gtm[:, :], op=ALU.mult), reads=["y1", "gtm"], writes=["y1"])
                DMA(y_out[tg * 128:(tg + 1) * 128, :], y1, [], r=["y1"])

                return pre_, st_
            for ti0 in range(0, TPB if stop != 'pre' else 0, 2):
                preA, stA = tile_block(ti0, 0)
                preB, stB = tile_block(ti0 + 1, 1)
                merged = []
                for ii in range(max(len(preA), len(preB))):
                    if ii < len(preA):
                        merged.append(preA[ii])
                    if ii < len(preB):
                        merged.append(preB[ii])
                tile_ops_ = merged + stA + stB
                npairs = max(1, TPB // 2)
                pi_ = ti0 // 2
                nx = nxt_pre[(len(nxt_pre) * pi_) // npairs:(len(nxt_pre) * (pi_ + 1)) // npairs]
                ia = 0; ib = 0
                ratio = (len(nx) + 1e-9) / max(1, len(tile_ops_))
                acc = 0.0
                for opx in tile_ops_:
                    S.op(opx[0], opx[1], opx[2], reads=opx[3], writes=opx[4], dma=opx[5], skip_pe=opx[6])
                    acc += ratio
                    while acc >= 1.0 and ib < len(nx):
                        o2 = nx[ib]; ib += 1; acc -= 1.0
                        S.op(o2[0], o2[1], o2[2], reads=o2[3], writes=o2[4], dma=o2[5], skip_pe=o2[6])
                while ib < len(nx):
                    o2 = nx[ib]; ib += 1
                    S.op(o2[0], o2[1], o2[2], reads=o2[3], writes=o2[4], dma=o2[5], skip_pe=o2[6])
            if bi % 2 == 1 and stop == 'all':
                g = bi // 2; gq = g % 2
                S.op("dve", "memset", dict(ap=bar[:, 0:1], constant=0.0), writes=["xn", "bar0", "rstd_b", "Qf"])
                S.op("pool", "memset", dict(ap=bar[:, 1:2], constant=0.0), reads=["xn"], writes=["bar1"])
                S.op("act", "activation", dict(out=bar[:, 2:3], in_=bar[:, 0:1], func=AF.Copy), reads=["xn", "bar0"], writes=["bar2"])
                nkb = 8 * g + 8
                Sbank = [[(S0_t[:, 0:512], "pMa"), (S0_t[:, 512:1024], "pSa")], [(S1_t[:, 0:512], "m1"), (S1_t[:, 512:1024], "m2")]]
                Sfull = [S0_t, S1_t]
                O_b = [(pO0_t, "pO0"), (pO1_t, "pO1")]
                Lb = m3f
                def qk_exp(j):
                    par = j % 2
                    for mp in range(2):
                        sbk, skey = Sbank[par][mp]
                        S.op("pe", "matmul", dict(out=sbk[:, :], lhsT=KT[:, j * 128:(j + 1) * 128], rhs=Qs[mp][:, gq, :], start=True, stop=False),
                             reads=["KT", "Qs%d" % mp], writes=[skey], skip_pe=True)
                        S.op("pe", "matmul", dict(out=sbk[:, :], lhsT=ones_b[:, :], rhs=qaug_pad[:, :], start=False, stop=True),
                             reads=["ones_b", "qaug_pad"], writes=[skey], skip_pe=True)
                    far = j < 8 * g
                    n = 8 * g - j + 7
                    if far:
                        S.op("act", "activation", dict(out=Pt[:, par, :, :].rearrange("p a b -> p (a b)"), in_=Sfull[par][:, :], func=AF.Exp, scale=0.125, bias=abias[:, n:n + 1]),
                             reads=[Sbank[par][0][1], Sbank[par][1][1], "abias"], writes=["Pt%d0" % par, "Pt%d1" % par])
                    for mp in range(2):
                        sbk, skey = Sbank[par][mp]
                        pk2 = "Pt%d%d" % (par, mp)
                        if not far:
                            jrel = j - 8 * g
                            S.op("dve", "scalar_tensor_tensor", dict(out=tn[:, mp, :], in0=sbk[:, :], scalar=0.125, in1=maskm[:, jrel, :], op0=ALU.mult, op1=ALU.add),
                                 reads=[skey, "maskm"], writes=["tn%d" % mp])
                            S.op("act", "activation", dict(out=Pt[:, par, mp, :], in_=tn[:, mp, :], func=AF.Exp, bias=abias[:, n:n + 1]), reads=["tn%d" % mp, "abias"], writes=[pk2])

                def pv_l(j):
                    par = j % 2
                    for mp in range(2):
                        pk2 = "Pt%d%d" % (par, mp)
                        S.op("pe", "matmul", dict(out=O_b[mp][0][:, :], lhsT=Vtm[:, j, :], rhs=Pt[:, par, mp, :], start=(j == 0), stop=(j == nkb - 1)), reads=[pk2, "Vtm"], writes=[O_b[mp][1]], skip_pe=True)
                    for mp in range(2):
                        pk2 = "Pt%d%d" % (par, mp)
                        if j < 8 * g:
                            if j == 0:
                                S.op("dve", "tensor_copy", dict(out=Lacc[mp], in_=Pt[:, par, mp, :]), reads=[pk2], writes=["Lacc%d" % mp])
                            else:
                                S.op("dve", "tensor_tensor", dict(out=Lacc[mp], in0=Lacc[mp], in1=Pt[:, par, mp, :], op=ALU.add), reads=[pk2, "Lacc%d" % mp], writes=["Lacc%d" % mp])
                        else:
                            S.op("pe", "matmul", dict(out=Lb[:, :], lhsT=esel[:, mp, :], rhs=Pt[:, par, mp, :], start=(j == 8 * g and mp == 0), stop=(g == 0 and j == nkb - 1 and mp == 1)), reads=[pk2, "esel"], writes=["m3"], skip_pe=True)

                qk_exp(0)
                for j in range(nkb):
                    if j + 1 < nkb:
                        qk_exp(j + 1)
                    pv_l(j)
                if g > 0:
                    for mp in range(2):
                        S.op("act", "activation", dict(out=Lhl[:, mp, 0, :], in_=Lacc[mp], func=AF.Copy), reads=["Lacc%d" % mp], writes=["Pt%d0" % mp, "Pt%d1" % mp])
                        S.op("dve", "tensor_tensor", dict(out=Lacc[mp], in0=Lacc[mp], in1=Lhl[:, mp, 0, :], op=ALU.subtract), reads=["Lacc%d" % mp, "Pt%d0" % mp, "Pt%d1" % mp], writes=["Lacc%d" % mp])
                        S.op("act", "activation", dict(out=Lhl[:, mp, 1, :], in_=Lacc[mp], func=AF.Copy), reads=["Lacc%d" % mp], writes=["Pt%d0" % mp, "Pt%d1" % mp])
                    for mp in range(2):
                        for hl in range(2):
                            S.op("pe", "matmul", dict(out=Lb[:, :], lhsT=esel[:, mp, :], rhs=Lhl[:, mp, hl, :], start=False, stop=(mp == 1 and hl == 1)), reads=["Pt%d0" % mp, "Pt%d1" % mp, "esel"], writes=["m3"], skip_pe=True)
                S.op("dve", "reciprocal", dict(out=rls[:, :], in_=Lb[0:2, :]), reads=["m3"], writes=["rls"])
                dma(rl_d[:, :], rls[:, :], ["rl_d"], r=["rls"])
                for mp in range(2):
                    dma(rb[:, mp, :], rl_d[mp:mp + 1, :].partition_broadcast(128), ["rb%d" % mp], r=["rl_d"])
                S.op("dve", "tensor_tensor", dict(out=oAB[:, 0, :], in0=pO0_t[:, :], in1=rb[:, 0, :], op=ALU.mult), reads=["pO0", "rb0"], writes=["oA"])
                S.op("dve", "scalar_tensor_tensor", dict(out=oAB[:, 1, :], in0=pO1_t[:, :], scalar=lamb[:, 0:1], in1=rb[:, 1, :], op0=ALU.mult, op1=ALU.mult), reads=["pO1", "rb1", "lamb"], writes=["oB"])
                S.op("pool", "tensor_tensor", dict(out=oAB[:, 0, :], in0=oAB[:, 0, :], in1=oAB[:, 1, :], op=ALU.add), reads=["oA", "oB"], writes=["oA"])
                S.op("act", "activation", dict(out=sqo[:, :], in_=oAB[:, 0, :], func=AF.Square), reads=["oA"], writes=["Pt00"])
                S.op("pe", "matmul", dict(out=pSa_t[:, :], lhsT=ones_b[:, :], rhs=sqo[:, :], start=True, stop=True), reads=["Pt00", "ones_b"], writes=["pSa"])
                S.op("act", "activation", dict(out=oAB[:, 1, :], in_=pSa_t[:, :], func=AF.Sqrt, scale=1.0 / 128, bias=eps_t[:, 2:3]), reads=["pSa", "eps"], writes=["oB"])
                S.op("dve", "reciprocal", dict(out=oAB[:, 1, :], in_=oAB[:, 1, :]), reads=["oB"], writes=["oB"])
                S.op("dve", "scalar_tensor_tensor", dict(out=oAB[:, 1, :], in0=oAB[:, 0, :], scalar=slc[:, 0:1], in1=oAB[:, 1, :], op0=ALU.mult, op1=ALU.mult), reads=["oA", "oB", "slc"], writes=["oB"])
                dma(o_out[:, g * 512:(g + 1) * 512], oAB[:, 1, :], ["oB"], r=["oB"])
                S.op("act", "activation", dict(out=bar[:, 3:4], in_=bar[:, 2:3], func=AF.Copy), reads=["tn0", "tn1", "oA", "oB", "bar2", "Lacc0", "Lacc1"], writes=["xn", "rstd_b", "Qf"])
        S.emit()
        pass
    return nc


D = 1024
DFF = 2816
NFF = 22
EPS = 1e-6


def build_B(ntok_halo=128, ntok=2048, BW=256):
    NT = ntok_halo + ntok
    nc = bass.Bass("TRN2", target_bir_lowering=False)
    mixT = nc.dram_tensor("mixT", [D, NT], F32, kind="ExternalInput").ap()
    xr = nc.dram_tensor("xr", [NT, D], F32, kind="ExternalInput").ap()
    w_out = nc.dram_tensor("w_out", [D, D], F32, kind="ExternalInput").ap()
    w_up = nc.dram_tensor("w_up", [D, 2 * DFF], F32, kind="ExternalInput").ap()
    w_down = nc.dram_tensor("w_down", [DFF, D], F32, kind="ExternalInput").ap()
    gbd = nc.dram_tensor("gb", [128, 2, D], F32, kind="ExternalInput").ap()
    g2d = nc.dram_tensor("g2c", [128, 8], F32, kind="ExternalInput").ap()
    cpd = nc.dram_tensor("convp", [128, 44, 4], F32, kind="ExternalInput").ap()
    outd = nc.dram_tensor("out", [ntok, D], F32, kind="ExternalOutput").ap()

    with ExitStack() as es:
        S = Sched(nc, es)
        S.pe_skip_all = True
        sb = lambda name, shape, dt: es.enter_context(nc.sbuf_tensor(name, shape, dt))
        ps = lambda name, shape, dt: es.enter_context(nc.psum_tensor(name, shape, dt))
        wup = sb("wup", [128, 8, 2 * DFF], BF16)
        wdn = sb("wdn", [128, NFF, D], BF16)
        wout = sb("wout", [128, 8, D], BF16)
        stg = [sb("stg%d" % i, [128, 1024], F32) for i in range(2)]
        mixb = sb("mixb", [128, 8, BW], BF16)
        h = sb("h", [128, BW // 128, D], F32)
        hn = sb("hn", [128, BW // 128, D], BF16)
        hnT = sb("hnT", [128, 8, BW], BF16)
        actT = sb("actT", [128, NFF, BW], BF16)
        ftmp = sb("ftmp", [128, 2, 4 * BW + 4], F32)
        gb = sb("gb_s", [128, 2, D], F32)
        g2c = sb("g2c_s", [128, 8], F32)
        cp = sb("cp", [128, 44, 4], F32)
        carry = sb("carry", [128, 44, 2], F32)
        ident = sb("ident", [128, 128], BF16)
        st = sb("st", [128, 16], F32)
        pA = [ps("pA%d" % i, [128, D], F32) for i in range(2)]
        pu = [ps("pu%d" % i, [128, 2, 256], F32) for i in range(2)]
        pt = ps("pt", [128, 8, 128], BF16)

        stg_i = [0]

        def load_convert(dst_ap, src_ap, shape_free, key, eng_rr=[0]):
            i = stg_i[0] % 2
            stg_i[0] += 1
            n = int(np.prod(shape_free))
            sv = stg[i][:, 0:n]
            if len(shape_free) == 2:
                sv = sv.rearrange("p (a b) -> p a b", b=shape_free[1])
            S.op("sp", "dma_start", dict(out=sv, in_=src_ap), writes=["stg%d" % i], dma=True)
            engs = ["act", "dve"]
            en = engs[eng_rr[0] % 2]
            eng_rr[0] += 1
            if en == "act":
                S.op("act", "activation", dict(out=dst_ap, in_=sv, func=AF.Copy), reads=["stg%d" % i], writes=[key])
            else:
                S.op(en, "tensor_copy", dict(out=dst_ap, in_=sv), reads=["stg%d" % i], writes=[key])

        S.op("sp", "dma_start", dict(out=gb[:, :, :], in_=gbd[:, :, :]), writes=["gb"], dma=True)
        S.op("sp", "dma_start", dict(out=g2c[:, :], in_=g2d[:, :]), writes=["g2c"], dma=True)
        S.op("sp", "dma_start", dict(out=cp[:, :, :], in_=cpd[:, :, :]), writes=["cp"], dma=True)
        S.op("pool", "memset", dict(ap=carry[:, :, :], constant=0.0), writes=["carry"])
        S.op("pool", "memset", dict(ap=ident[:, :], constant=1.0), writes=["ident"])
        S.op("pool", "affine_select", dict(out=ident[:, :], in_=ident[:, :], pattern=[[-1, 128]],
                                               compare_op=ALU.is_equal, fill=0.0, base=0, channel_multiplier=1),
             reads=["ident"], writes=["ident"])
        for k in range(8):
            load_convert(wout[:, k, :], w_out[k * 128:(k + 1) * 128, :], [1024], "wout")
        for k in range(8):
            for c in range(0, 2 * DFF, 1024):
                w = min(1024, 2 * DFF - c)
                load_convert(wup[:, k, c:c + w], w_up[k * 128:(k + 1) * 128, c:c + w], [w], "wup")
        for j in range(NFF):
            load_convert(wdn[:, j, :], w_down[j * 128:(j + 1) * 128, :], [1024], "wdn")

        def rstd_from(ss_ap, n_el, eps, key_in, key_out, out_ap):
            S.op("act", "activation", dict(out=out_ap, in_=ss_ap, func=AF.Sqrt, scale=1.0 / n_el, bias=eps_t[:, 0:1]),
                 reads=[key_in, "eps"], writes=[key_out])
            S.op("dve", "reciprocal", dict(out=out_ap, in_=out_ap), reads=[key_out], writes=[key_out])

        eps_t = sb("eps_t", [128, 1], F32)
        S.op("pool", "memset", dict(ap=eps_t[:, :], constant=EPS), writes=["eps"])

        batches = []
        t0 = 0
        if ntok_halo:
            batches.append((0, ntok_halo, True))
            t0 = ntok_halo
        while t0 < NT:
            batches.append((t0, min(BW, NT - t0), False))
            t0 += BW
        out_i = [0]
        for (t0, W, is_halo) in batches:
            nt = W // 128
            for hf in range(2):
                load_convert(mixb[:, hf * 4:(hf + 1) * 4, 0:W],
                             mixT[hf * 512:(hf + 1) * 512, t0:t0 + W].rearrange("(k p) t -> p k t", p=128),
                             [4, W], "mixb")
            S.op("sp", "dma_start", dict(out=h[:, 0:nt, :], in_=xr[t0:t0 + W, :].rearrange("(j p) d -> p j d", p=128)),
                 writes=["h"], dma=True)
            for j in range(nt):
                pa = pA[j % 2]; pk = "pA%d" % (j % 2)
                for hf in range(2):
                    for k in range(8):
                        S.op("pe", "matmul", dict(out=pa[:, hf * 512:(hf + 1) * 512], lhsT=mixb[:, k, j * 128:(j + 1) * 128],
                                                            rhs=wout[:, k, hf * 512:(hf + 1) * 512], start=(k == 0), stop=(k == 7)),
                             reads=["mixb", "wout"], writes=[pk])
                ti = stg_i[0] % 2; stg_i[0] += 1
                tmp = stg[ti][:, :]; tk = "stg%d" % ti
                junk = tmp[:, 0:512]
                for hf in range(2):
                    S.op("act", "activation", dict(out=junk, in_=pa[:, hf * 512:(hf + 1) * 512], func=AF.Square, accum_out=st[:, hf:hf + 1]),
                         reads=[pk], writes=[tk, "st%d" % hf])
                S.op("dve", "tensor_tensor", dict(out=st[:, 2:3], in0=st[:, 0:1], in1=st[:, 1:2], op=ALU.add), reads=["st0", "st1"], writes=["st2"])
                rstd_from(st[:, 2:3], D, EPS, "st2", "st3", st[:, 3:4])
                S.op("dve", "scalar_tensor_tensor", dict(out=tmp, in0=pa[:, :], scalar=st[:, 3:4], in1=gb[:, 0, :], op0=ALU.mult, op1=ALU.mult),
                     reads=[pk, "st3", "gb"], writes=[tk])
                S.op("pool", "tensor_tensor", dict(out=h[:, j, :], in0=h[:, j, :], in1=tmp, op=ALU.add), reads=["h", tk], writes=["h"])
                S.op("act", "activation", dict(out=tmp, in_=h[:, j, :], func=AF.Square, accum_out=st[:, 4:5]),
                     reads=["h"], writes=[tk, "st4"])
                rstd_from(st[:, 4:5], D, EPS, "st4", "st5", st[:, 5:6])
                S.op("dve", "tensor_scalar", dict(out=hn[:, j, :], in0=h[:, j, :], scalar1=st[:, 5:6], scalar2=None, op0=ALU.mult),
                     reads=["h", "st5"], writes=["hn"])
                for k in range(8):
                    S.op("pe", "transpose", dict(out=pt[:, k, :], in_=hn[:, j, k * 128:(k + 1) * 128], identity=ident[:, :]),
                         reads=["hn", "ident"], writes=["pt"])
                S.op("dve", "tensor_tensor", dict(out=hnT[:, :, j * 128:(j + 1) * 128], in0=pt[:, :, :],
                                                       in1=g2c[:, :].unsqueeze(2).broadcast_to([128, 8, 128]), op=ALU.mult),
                     reads=["pt", "g2c"], writes=["hnT"])
            for jp in range(NFF):
                s = jp % 2
                jg, jv = jp, NFF + jp
                puk = "pu%d" % s
                for which, jc in ((0, jg), (1, jv)):
                    for k in range(8):
                        S.op("pe", "matmul", dict(out=pu[s][:, which, 0:W], lhsT=wup[:, k, jc * 128:(jc + 1) * 128],
                                                                             rhs=hnT[:, k, 0:W], start=(k == 0), stop=(k == 7)),
                             reads=["wup", "hnT"], writes=[puk])
                fs = "f%d" % s
                ugs = ftmp[:, s, 0:BW + 2]; uvs = ftmp[:, s, BW + 2:2 * BW + 4]
                cg = ftmp[:, s, 2 * BW + 4:3 * BW + 4]; cv = ftmp[:, s, 3 * BW + 4:4 * BW + 4]; gel = cg
                for which, jc, ub, cb, nm in ((0, jg, ugs, cg, "g"), (1, jv, uvs, cv, "v")):
                    ku = fs + "u" + nm; kc = fs + "c" + nm
                    S.op("act", "activation", dict(out=ub[:, 2:2 + W], in_=pu[s][:, which, 0:W], func=AF.Identity),
                         reads=[puk], writes=[ku])
                    S.op("pool", "tensor_copy", dict(out=ub[:, 0:2], in_=carry[:, jc, :]), reads=["carry%d" % jc], writes=[ku + "c"])
                    if not is_halo:
                        S.op("act", "activation", dict(out=cb[:, 0:W], in_=pu[s][:, which, 0:W], func=AF.Identity,
                                                                                  scale=cp[:, jc, 2:3], bias=cp[:, jc, 3:4]),
                             reads=[puk, "cp"], writes=[kc])
                        S.op("dve", "scalar_tensor_tensor", dict(out=cb[:, 0:W], in0=ub[:, 1:1 + W], scalar=cp[:, jc, 1:2], in1=cb[:, 0:W],
                                                                                 op0=ALU.mult, op1=ALU.add),
                             reads=[ku, ku + "c", kc, "cp"], writes=[kc])
                        S.op("dve", "scalar_tensor_tensor", dict(out=cb[:, 0:W], in0=ub[:, 0:W], scalar=cp[:, jc, 0:1], in1=cb[:, 0:W],
                                                                                 op0=ALU.mult, op1=ALU.add),
                             reads=[ku, ku + "c", kc, "cp"], writes=[kc])
                    S.op("pool", "tensor_copy", dict(out=carry[:, jc, :], in_=ub[:, W:W + 2]), reads=[ku, ku + "c"], writes=["carry%d" % jc])
                if not is_halo:
                    S.op("act", "activation", dict(out=gel[:, 0:W], in_=cg[:, 0:W], func=AF.Gelu_apprx_tanh),
                         reads=[fs + "cg"], writes=[fs + "cg"])
                    S.op("pool", "tensor_tensor", dict(out=actT[:, jp, 0:W], in0=gel[:, 0:W], in1=cv[:, 0:W], op=ALU.mult),
                         reads=[fs + "cg", fs + "cv"], writes=["actT"])
            if is_halo:
                continue
            for j in range(nt):
                pa = pA[j % 2]; pk = "pA%d" % (j % 2)
                for hf in range(2):
                    for jp in range(NFF):
                        S.op("pe", "matmul", dict(out=pa[:, hf * 512:(hf + 1) * 512], lhsT=actT[:, jp, j * 128:(j + 1) * 128],
                                                              rhs=wdn[:, jp, hf * 512:(hf + 1) * 512], start=(jp == 0), stop=(jp == NFF - 1)),
                             reads=["actT", "wdn"], writes=[pk])
                oi = stg_i[0] % 2; stg_i[0] += 1
                ob = stg[oi]; ok_ = "stg%d" % oi
                junk = ob[:, 0:512]
                for hf in range(2):
                    S.op("act", "activation", dict(out=junk, in_=pa[:, hf * 512:(hf + 1) * 512], func=AF.Square, accum_out=st[:, 6 + hf:7 + hf]),
                         reads=[pk], writes=[ok_, "st%d" % (6 + hf)])
                S.op("dve", "tensor_tensor", dict(out=st[:, 8:9], in0=st[:, 6:7], in1=st[:, 7:8], op=ALU.add), reads=["st6", "st7"], writes=["st8"])
                rstd_from(st[:, 8:9], D, EPS, "st8", "st9", st[:, 9:10])
                S.op("dve", "scalar_tensor_tensor", dict(out=ob[:, :], in0=pa[:, :], scalar=st[:, 9:10], in1=gb[:, 1, :], op0=ALU.mult, op1=ALU.mult),
                     reads=[pk, "st9", "gb"], writes=[ok_])
                S.op("pool", "tensor_tensor", dict(out=ob[:, :], in0=h[:, j, :], in1=ob[:, :], op=ALU.add), reads=["h", ok_], writes=[ok_])
                r0 = t0 - ntok_halo + j * 128
                S.op("sp", "dma_start", dict(out=outd[r0:r0 + 128, :], in_=ob[:, :]), reads=["stg%d" % oi], dma=True)
        S.emit()
        pass
    return nc


def prep_A(d, c, S_tok, xT):
    f = np.float32
    h = c; dh = c // 2; p = c % 2
    w_in = d["w_in"][0]
    cols = np.concatenate([np.arange(64 * h, 64 * h + 64), 512 + np.arange(64 * h, 64 * h + 64), 1024 + np.arange(64 * h, 64 * h + 64),
                           np.arange(1536, 1600), np.arange(1600, 1664), np.arange(1664, 1792),
                           1792 + dh * 128 + np.arange(128), 1792 + 512 + dh * 128 + np.arange(128), 1792 + 1024 + dh * 128 + np.arange(128)])
    wA = np.ascontiguousarray(w_in[:, cols])
    mu = d["mu_shift"][0]
    hs = slice(64 * h, 64 * h + 64)
    pv64 = np.zeros((64, 16), f)
    for i in range(5):
        pv64[:, i] = mu[cols[64 * i:64 * i + 64]]
    pv64[:, 5] = d["w_decay0"][0][hs]; pv64[:, 6] = d["w_iclr0"][0][hs]; pv64[:, 7] = d["k_k"][0][hs]; pv64[:, 8] = d["k_a"][0][hs]
    pv64[:, 9] = d["r_k"][0][h]
    pv128 = np.zeros((128, 4), f); pv128[:, 0] = mu[1664:1792]; pv128[:, 1] = d["diff_subln"][0]
    bc64 = np.zeros((128, 3, 64), f); bc64[:, 0] = d["ln_x_w"][0][hs][None]; bc64[:, 1] = d["ln_x_b"][0][hs][None]
    lamv = np.stack([d["lambda_q1"][0], d["lambda_k1"][0], d["lambda_q2"][0], d["lambda_k2"][0]])[None].astype(f)
    lamv = np.ascontiguousarray(np.broadcast_to(lamv, (128, 4, 64)))
    i = np.arange(128)[:, None]; t = np.arange(128)[None, :]
    same = (i // 64) == (t // 64)
    masks = np.stack([same & (i < t), same & (i <= t), same & (i < t), same & (i <= t), same & (t < i)], 1).astype(f)
    slope = 2.0 ** (-8.0 * (dh + 1) / 4)
    kl = np.arange(128)[:, None].astype(np.float64); n1 = np.arange(136)[None, :].astype(np.float64)
    abias = (slope * (kl - 128.0 * (n1 - 7))).astype(f)
    maskm = np.zeros((128, 8, 4, 128), f)
    klq = np.arange(128)[:, None]; qlq = np.arange(128)[None, :]
    for jr in range(8):
        for ib in range(4):
            tq = p + 2 * ib
            if jr > tq:
                maskm[:, jr, ib, :] = -30000.0
            elif jr == tq:
                maskm[:, jr, ib, :] = np.where(klq <= qlq, 0.0, -30000.0)
    maskm = np.ascontiguousarray(maskm.reshape(128, 8, 512))
    qaug = np.repeat(np.array([-slope * 1024.0 * (p + 2 * ib) for ib in range(4)], f), 128)[None, :].astype(f)
    psel = np.ascontiguousarray(np.broadcast_to(np.array([1.0 - p, p], f)[None], (128, 2)))
    g1c = np.ascontiguousarray(d["ln_attn_pre"][0].reshape(8, 128).T)
    return {"xT": xT, "wA": wA, "g1c": g1c, "pv64": pv64, "pv128": pv128,
            "wdu": np.ascontiguousarray(d["w_decay_up"][0][:, hs]), "wiu": np.ascontiguousarray(d["w_iclr_up"][0][:, hs]),
            "wgu": np.ascontiguousarray(d["w_gate_up"][0][:, hs]), "bc64": bc64, "lamv": lamv,
            "masks": np.ascontiguousarray(masks), "maskm": maskm, "qaug": qaug, "abias": abias, "psel": psel}


def kernel(**inputs):
    d = {k: np.asarray(v) for k, v in inputs.items()}
    S_tok = 16384
    x = d["x"][0]
    xT = np.ascontiguousarray(x.T)
    ncA = build_A(S_tok)
    insA = [prep_A(d, c, S_tok, xT) for c in range(8)]
    resA = run_bass_kernel_spmd(ncA, insA, core_ids=list(range(8)))
    mix = np.empty((S_tok, 1024), np.float32)
    for c in range(8):
        mix[:, 64 * c:64 * c + 64] = resA.results[c]["y_rwkv"]
        o = resA.results[c]["oT_diff"].T.reshape(S_tok // 256, 128, 128)
        dh, p = c // 2, c % 2
        mix[:, 512 + 128 * dh:512 + 128 * dh + 128].reshape(S_tok // 128, 128, 128)[p::2] = o
    ncB = build_B()
    def bcast(v):
        return np.broadcast_to(v[None, :], (128, 1024))
    gb = np.ascontiguousarray(np.stack([bcast(d["ln_attn_post"][0]), bcast(d["ln_ffn_post"][0])], 1)).astype(np.float32)
    g2c = np.ascontiguousarray(d["ln_ffn_pre"][0].reshape(8, 128).T)
    convp = np.ascontiguousarray(np.concatenate([d["conv_w"][0], d["conv_b"][0][None]], 0).reshape(4, 44, 128).transpose(2, 1, 0))
    insB = []
    for c in range(8):
        lo = 2048 * c - 128
        if c == 0:
            m = np.concatenate([np.zeros((128, 1024), np.float32), mix[0:2048]], 0)
            xx = np.concatenate([np.zeros((128, 1024), np.float32), x[0:2048]], 0)
        else:
            m = mix[lo:lo + 2176]; xx = x[lo:lo + 2176]
        insB.append({"mixT": np.ascontiguousarray(m.T), "xr": np.ascontiguousarray(xx), "w_out": d["w_out"][0], "w_up": d["w_up"][0],
                     "w_down": d["w_down"][0], "gb": gb, "g2c": g2c, "convp": convp})
    resB = run_bass_kernel_spmd(ncB, insB, core_ids=list(range(8)))
    out = np.concatenate([r["out"] for r in resB.results], 0)[None]
    return out.astype(np.float32)
```

```python
import math
import numpy as np
from contextlib import ExitStack
import concourse.bass as bass
import concourse.mybir as mybir
from concourse.bass_utils import run_bass_kernel_spmd

F32 = mybir.dt.float32
BF16 = mybir.dt.bfloat16
AF = mybir.ActivationFunctionType
ALU = mybir.AluOpType
AX = mybir.AxisListType

N_DMA_SEMS = 24


class Sched:
    ENGS = ("pe", "act", "dve", "pool", "sp")

    def __init__(self, nc, es):
        self.nc = nc
        self.es = es
        self.h = {"pe": nc.tensor, "act": nc.scalar, "dve": nc.vector, "pool": nc.gpsimd, "sp": nc.sync}
        self.ops = {k: [] for k in self.ENGS}
        self.cnt = {}
        self.sems = {}
        for k in self.ENGS:
            self.sems["e_" + k] = es.enter_context(nc.semaphore("e_" + k))
            self.cnt["e_" + k] = 0
        for i in range(N_DMA_SEMS):
            self.sems["d%d" % i] = es.enter_context(nc.semaphore("d%d" % i))
            self.cnt["d%d" % i] = 0
        self.dma_rr = 0
        self.last_w = {}
        self.readers = {}
        self.seen = {k: {} for k in self.ENGS}
        self.nwaits = 0
        self.alias = {}
        self.pe_skip_all = False
        self._skip_pe = False
        self.pe_plain = {}
        self.plain_chain = False
        self.exclusive = set()
        import os
        self.limit = int(os.environ.get('OPLIMIT', '0')) or None
        self.total = 0

    def _need(self, eng, ev, waits):
        if ev is None:
            return
        s, v = ev
        if eng == 'pe' and s == 'e_pe' and (self.pe_skip_all or (self._skip_pe and (not self.plain_chain or self.pe_plain.get(v, False)))):
            return
        if self.seen[eng].get(s, 0) >= v:
            return
        cur = waits.get(s, 0)
        if v > cur:
            waits[s] = v

    def op(self, eng, meth, kw, reads=(), writes=(), dma=False, skip_pe=False):
        self._skip_pe = skip_pe
        self.total += 1
        if self.limit is not None and self.total > self.limit:
            return None
        fn = (meth, kw)
        reads = [self.alias.get(r, r) for r in reads]
        writes = [self.alias.get(w, w) for w in writes]
        writes = writes + [r for r in reads if r in self.exclusive and r not in writes]
        reads = [r for r in reads if r not in self.exclusive]
        waits = {}
        for r in reads:
            self._need(eng, self.last_w.get(r), waits)
        for w in writes:
            self._need(eng, self.last_w.get(w), waits)
            for ev in self.readers.get(w, ()):
                self._need(eng, ev, waits)
        if dma:
            s = "d%d" % self.dma_rr
            self.dma_rr = (self.dma_rr + 1) % N_DMA_SEMS
            if self.cnt[s] > 0:
                self._need(eng, (s, self.cnt[s]), waits)
            self.cnt[s] += 16
            ev = (s, self.cnt[s])
            inc = 16
        else:
            s = "e_" + eng
            self.cnt[s] += 1
            ev = (s, self.cnt[s])
            inc = 1
            if eng == 'pe':
                self.pe_plain[self.cnt[s]] = bool(skip_pe)
        for s_, v_ in waits.items():
            self.seen[eng][s_] = v_
        self.nwaits += len(waits)
        self.ops[eng].append((fn, list(waits.items()), s, inc))
        for w in writes:
            self.last_w[w] = ev
            self.readers[w] = []
        for r in reads:
            self.readers.setdefault(r, []).append(ev)
        return ev

    def emit(self, final_waits_engine="sp"):
        nc = self.nc
        fin = []
        for s, c in self.cnt.items():
            if c > 0:
                fin.append((s, c))
        with nc.Block() as block:
            def mk(engname):
                def body(e):
                    for fn, waits, s, inc in self.ops[engname]:
                        for ws, wv in waits:
                            e.wait_ge(self.sems[ws], wv)
                        ins = getattr(e, fn[0])(**fn[1])
                        ins.then_inc(self.sems[s], inc)
                    if engname == final_waits_engine:
                        for ws, wv in fin:
                            e.wait_ge(self.sems[ws], wv)
                return body
            block.tensor(mk("pe"))
            block.scalar(mk("act"))
            block.vector(mk("dve"))
            block.gpsimd(mk("pool"))
            block.sync(mk("sp"))


D = 1024
CDEC = math.exp(-0.5)
NCOL = 64 * 5 + 128 * 4
C_R, C_K, C_V, C_WL, C_AL, C_GL, C_DQ, C_DK, C_DV = 0, 64, 128, 192, 256, 320, 448, 576, 704


def build_A(S_tok=16384, BT=512, stop='all'):
    nc = bass.Bass("TRN2", target_bir_lowering=False)
    NTILE = S_tok // 128
    NB = S_tok // BT
    TPB = BT // 128
    dt_in = lambda name, shape: nc.dram_tensor(name, shape, F32, kind="ExternalInput").ap()
    xT = dt_in("xT", [D, S_tok])
    wA = dt_in("wA", [D, NCOL])
    g1d = dt_in("g1c", [128, 8])
    pv64 = dt_in("pv64", [64, 16])
    pv128 = dt_in("pv128", [128, 4])
    wdu_d = dt_in("wdu", [64, 64]); wiu_d = dt_in("wiu", [64, 64]); wgu_d = dt_in("wgu", [128, 64])
    bc64 = dt_in("bc64", [128, 3, 64])
    lamd = dt_in("lamv", [128, 4, 64])
    masks_d = dt_in("masks", [128, 5, 128])
    maskm_d = dt_in("maskm", [128, 8, 512])
    qaug_d = dt_in("qaug", [1, 512])
    abias_d = dt_in("abias", [128, 136])
    psel_d = dt_in("psel", [128, 2])
    rl_d = nc.dram_tensor("rl_d", [2, 512], F32).ap()
    y_out = nc.dram_tensor("y_rwkv", [S_tok, 64], F32, kind="ExternalOutput").ap()
    o_out = nc.dram_tensor("oT_diff", [128, S_tok // 2], F32, kind="ExternalOutput").ap()

    with ExitStack() as es:
        S = Sched(nc, es)
        S.exclusive = {"pj0", "pMa", "pO0", "pO1", "pSa", "m1", "m2", "m3"}
        S.alias = {"pMb": "m1", "pZ": "m1", "pX": "m1", "pPQ": "m2", "pG": "m2", "pY": "m2", "pY2": "m2", "pT": "m3", "pSt": "pO0", "pgt": "pO0", "prk": "pO0", "pSa0": "pSa", "pSa1": "pSa", "rb0": "tn0", "rb1": "tn1"}
        sb = lambda name, shape, dt: es.enter_context(nc.sbuf_tensor(name, shape, dt))
        ps = lambda name, shape, dt: es.enter_context(nc.psum_tensor(name, shape, dt))
        wAb = sb("wAb", [128, 8, NCOL], BF16)
        stg0_ = sb("stgA0", [128, 8, BT], F32)
        stg = [stg0_, stg0_]
        xn = sb("xn", [128, 8, BT], BF16)
        sq = xn
        rstd_b = sb("rstd_b", [128, BT], F32)
        g1c = sb("g1c_s", [128, 8], F32)
        p64 = sb("p64", [64, 16], F32); p128 = sb("p128", [128, 4], F32)
        omm64 = sb("omm64", [64, 8], F32)
        omm128 = sb("omm128", [128, 2], F32)
        wdu = sb("wdu_s", [64, 64], BF16); wiu = sb("wiu_s", [64, 64], BF16); wgu = sb("wgu_s", [128, 64], BF16)
        wtmp = sb("wtmp", [128, 64], F32)
        b64 = sb("b64", [128, 3, 64], F32); slc = sb("slc", [128, 1], F32)
        lamt = sb("lamt", [128, 4, 64], F32); lams = sb("lams", [128, 8], F32); lamb = sb("lamb", [128, 2], F32)
        masks = sb("masks_s", [128, 5, 128], F32); maskm = sb("maskm_s", [128, 8, 512], BF16)
        rls = sb("rls", [2, 512], F32); qaug_f = rls[0:1, :]; esel = sb("esel", [128, 2, 128], BF16)
        abias = sb("abias_s", [128, 136], F32); psel = sb("psel_s", [128, 2], F32)
        ones_b = sb("ones_b", [128, 128], BF16); ident_b = sb("ident_b", [128, 128], BF16); ident_f = sb("ident_f", [64, 64], F32)
        rmask = sb("rmask", [64, BT], F32)
        eps_t = sb("eps_t", [128, 4], F32)
        praw = sb("praw", [64, 5, BT + 1], F32); graw = sb("graw", [128, BT + 1], F32)
        xs = sb("xs", [64, 3, BT], F32)
        gls = sb("gls", [128, BT], BF16)
        tmp64 = sb("tmp64", [64, 6, BT], F32)
        tmp64b = sb("tmp64b", [64, 3, BT], F32)
        bf64 = sb("bf64", [64, 3, BT], BF16)
        fmops = sb("fmops", [64, TPB, 5, 128], BF16)
        gL = sb("gL", [64, BT // 64 + 1], F32)
        KT = sb("KT", [128, S_tok], BF16)
        Vtm = sb("Vtm", [128, NTILE, 128], BF16)
        Qf = sb("Qf", [128, BT], F32)
        Qs = [sb("Qs%d" % i, [128, 2, 512], BF16) for i in range(2)]; qtmp = sb("qtmp", [128, 2, 128], F32)
        qaug_pad = sb("qaug_pad", [128, 512], BF16)
        dvb = sb("dvb", [128, BT], BF16)
        Msb = sb("Msb", [128, 5, 128], BF16)
        tmops = sb("tmops", [128, 4, 64], BF16)
        Zf = sb("Zf", [128, 128], F32); Zb = sb("Zb", [128, 128], BF16)
        XY = [sb("XY%d" % i, [128, 2, 128], BF16) for i in range(2)]
        Lc = sb("Lc", [64, 2, 64], BF16); Qc = sb("Qc", [64, 2, 64], F32)
        Gz = sb("Gz", [64, 2, 128], BF16)
        Yl = sb("Yl", [128, 64], F32)
        Rst = [sb("Rst%d" % i, [64, 64], BF16) for i in range(3)]
        gtm = sb("gtm", [128, 64], F32); rkk = sb("rkk", [128, 1], F32)
        yb = sb("yb", [128, 2, 64], F32); stt = sb("stt", [128, 16], F32); bns = sb("bns", [128, 6], F32)
        Pt = sb("Pt", [128, 2, 2, 512], BF16)
        tn = xn[:, 0:4, :].rearrange("p a b -> p (a b)").bitcast(F32).rearrange("p (a b) -> p a b", b=512)
        oAB = xn[:, 4:8, :].rearrange("p a b -> p (a b)").bitcast(F32).rearrange("p (a b) -> p a b", b=512)
        bar = sb("bar", [128, 4], F32)
        Lacc = [rstd_b[:, :], Qf[:, :]]
        rb = tn
        sqo = Pt[:, 0, 0, :]
        Lhl = Pt
        pj0_t = ps("pj0", [128, 512], F32)
        pMa = ps("pMa", [128, 4, 128], F32)
        pO0_t = ps("pO0", [128, 512], F32); pO1_t = ps("pO1", [128, 512], F32)
        pO = [pO0_t, pO1_t]
        pSa_t = ps("pSa", [128, 512], F32)
        m1 = ps("m1", [128, 512], F32); m2 = ps("m2", [128, 512], F32); m3f = ps("m3", [128, 512], F32); m3 = m3f[:, :].bitcast(BF16)
        pj = [m3f, m3f]
        pMb = m1[:, 0:128]; pZ = m1[:, 128:256]; pX = m1[:, 256:512].rearrange("p (a b) -> p a b", b=128)
        pPQ = m2[0:64, 0:256].rearrange("p (a b) -> p a b", b=64); pG = m2[0:64, 256:384]; pY = m2[:, 384:448]; pY2 = m2[:, 384:448]
        pT = pO0_t[:, 0:128].bitcast(BF16).rearrange("p (a b) -> p a b", b=64)
        pSt = pO0_t[0:64, 256:320]; pgt = pO0_t[:, 384:448]; prk = m2[:, 448:452]

        FM = [fmops, sb("fmops_b", [64, TPB, 5, 128], BF16)]
        GL = [gL, sb("gL_b", [64, BT // 64 + 1], F32)]
        GLS = [gls, sb("gls_b", [128, BT], BF16)]
        DVB = [dvb, sb("dvb_b", [128, BT], BF16)]
        RK = [sb("rk_a", [64, BT], BF16), sb("rk_b", [64, BT], BF16)]
        LOCALKEYS = {"Msb", "Msb4", "tmops", "Zf", "Zb", "XY0", "XY1", "Lc", "Qc", "Gz", "Yl", "gtm", "rkk", "y0", "y1", "stt0", "stt1", "stt2", "stt3",
                     "pMa", "pMb", "pZ", "pX", "pT", "pPQ", "pG", "pY", "pY2", "pSt", "pgt", "prk"}
        Msb_b = sb("Msb_b", [128, 5, 128], BF16); tmops_b = sb("tmops_b", [128, 4, 64], BF16)
        Zf_b = sb("Zf_b", [128, 128], F32); Zb_b = sb("Zb_b", [128, 128], BF16)
        XY_b = [sb("XYb%d" % i, [128, 2, 128], BF16) for i in range(2)]
        Lc_b = sb("Lc_b", [64, 2, 64], BF16); Qc_b = sb("Qc_b", [64, 2, 64], F32); Gz_b = sb("Gz_b", [64, 2, 128], BF16)
        Yl_b = sb("Yl_b", [128, 64], F32); gtm_b = sb("gtm_b", [128, 64], F32); rkk_b = sb("rkk_b", [128, 1], F32)
        yb_b = sb("yb_b", [128, 2, 64], F32); stt_b = sb("stt_b", [128, 16], F32)
        TB = [(Msb, tmops, Zf, Zb, XY, Lc, Qc, Gz, Yl, gtm, rkk, yb, stt),
              (Msb_b, tmops_b, Zf_b, Zb_b, XY_b, Lc_b, Qc_b, Gz_b, Yl_b, gtm_b, rkk_b, yb_b, stt_b)]
        TP = [(pMa, pMb, pZ, pX, pT, pPQ, pG, pY, pY2, pSt, pgt, prk),
              (pSa_t[:, :].rearrange("p (a b) -> p a b", b=128), pO1_t[:, 0:128], pO1_t[:, 128:256], pO1_t[:, 256:512].rearrange("p (a b) -> p a b", b=128),
               pO0_t[:, 128:256].bitcast(BF16).rearrange("p (a b) -> p a b", b=64),
               pj0_t[0:64, 0:256].rearrange("p (a b) -> p a b", b=64), pj0_t[0:64, 256:384], pj0_t[:, 384:448], pj0_t[:, 384:448],
               pO0_t[0:64, 320:384], pO0_t[:, 448:512], pj0_t[:, 448:452])]
        for k_, v_ in (("pMa", "pMa"), ("pMb", "m1"), ("pZ", "m1"), ("pX", "m1"), ("pPQ", "m2"), ("pG", "m2"), ("pY", "m2"), ("pY2", "m2"), ("pT", "pO0"), ("pSt", "pO0"), ("pgt", "pO0"), ("prk", "m2")):
            S.alias[k_ + "@0"] = v_
        for k_, v_ in (("pMa", "pSa"), ("pMb", "pO1"), ("pZ", "pO1"), ("pX", "pO1"), ("pPQ", "pj0"), ("pG", "pj0"), ("pY", "pj0"), ("pY2", "pj0"), ("pT", "pO0"), ("pSt", "pO0"), ("pgt", "pO0"), ("prk", "pj0")):
            S.alias[k_ + "@1"] = v_

        def dma(out, in_, w, r=()):
            S.op("sp", "dma_start", dict(out=out, in_=in_), reads=list(r), writes=list(w), dma=True)

        for dst, src, k in ((g1c, g1d, "g1c"), (p64, pv64, "p64"), (p128, pv128, "p128"), (b64, bc64, "b64"),
                            (lamt, lamd, "lamt"), (masks, masks_d, "masks"), (abias, abias_d, "abias"), (psel, psel_d, "psel")):
            nd = len(src.shape)
            sl = tuple(slice(None) for _ in range(nd))
            dma(dst[sl], src[sl], [k])
        dma(qaug_f[:, :], qaug_d[:, :], ["rls"])
        S.op("pool", "memset", dict(ap=qaug_pad[:, :], constant=0.0), writes=["qaug_pad"])
        S.op("dve", "tensor_copy", dict(out=qaug_pad[0:1, :], in_=qaug_f[:, :]), reads=["rls"], writes=["qaug_pad"])
        for mp_ in range(2):
            S.op("pool", "memset", dict(ap=Qs[mp_][:, :, :], constant=0.0), writes=["Qs%d" % mp_])
        for jr_ in range(8):
            dma(stg0_[:, 0, :], maskm_d[:, jr_, :], ["stg0"])
            S.op("dve", "tensor_copy", dict(out=maskm[:, jr_, :], in_=stg0_[:, 0, :]), reads=["stg0"], writes=["maskm"])
        S.op("pool", "memset", dict(ap=esel[:, :, :], constant=0.0), writes=["esel"])
        S.op("pool", "memset", dict(ap=esel[:, 0, 0:1], constant=1.0), writes=["esel"])
        S.op("pool", "memset", dict(ap=esel[:, 1, 1:2], constant=1.0), writes=["esel"])
        for dstw, srcw, rows in ((wdu, wdu_d, 64), (wiu, wiu_d, 64), (wgu, wgu_d, 128)):
            dma(wtmp[0:rows, :], srcw[:, :], ["wtmp"])
            S.op("dve", "tensor_copy", dict(out=dstw[:, :], in_=wtmp[0:rows, :]), reads=["wtmp"], writes=["wsm"])
        S.op("pool", "memset", dict(ap=ones_b[:, :], constant=1.0), writes=["ones_b"])
        S.op("pool", "memset", dict(ap=ident_b[:, :], constant=1.0), writes=["ident_b"])
        S.op("pool", "affine_select", dict(out=ident_b[:, :], in_=ident_b[:, :], pattern=[[-1, 128]], compare_op=ALU.is_equal, fill=0.0, base=0, channel_multiplier=1),
             reads=["ident_b"], writes=["ident_b"])
        S.op("pool", "memset", dict(ap=ident_f[:, :], constant=1.0), writes=["ident_f"])
        S.op("pool", "affine_select", dict(out=ident_f[:, :], in_=ident_f[:, :], pattern=[[-1, 64]], compare_op=ALU.is_equal, fill=0.0, base=0, channel_multiplier=1),
             reads=["ident_f"], writes=["ident_f"])
        S.op("pool", "memset", dict(ap=rmask[:, :], constant=1.0), writes=["rmask"])
        S.op("pool", "memset", dict(ap=rmask[:, :].rearrange("p (c l) -> p c l", l=64)[:, :, 0:1], constant=0.0), writes=["rmask"])
        S.op("pool", "memset", dict(ap=eps_t[:, 0:1], constant=1e-6), writes=["eps"])
        S.op("pool", "memset", dict(ap=eps_t[:, 1:2], constant=64e-5), writes=["eps"])
        S.op("pool", "memset", dict(ap=eps_t[:, 2:3], constant=1e-5), writes=["eps"])
        S.op("pool", "memset", dict(ap=eps_t[:, 3:4], constant=1e-24), writes=["eps"])
        S.op("pool", "memset", dict(ap=praw[:, :, 0:1], constant=0.0), writes=["praw%d" % i for i in range(5)])
        S.op("pool", "memset", dict(ap=graw[:, 0:1], constant=0.0), writes=["graw"])
        S.op("pool", "memset", dict(ap=Gz[:, :, :], constant=0.0), writes=["Gz@0"])
        S.op("pool", "memset", dict(ap=Gz_b[:, :, :], constant=0.0), writes=["Gz@1"])
        S.op("pool", "memset", dict(ap=Rst[0][:, :], constant=0.0), writes=["R0"])
        S.op("pool", "memset", dict(ap=gL[:, 0:1], constant=1.0), writes=["gL_0"])
        S.op("dve", "tensor_scalar", dict(out=omm64[:, 0:5], in0=p64[:, 0:5], scalar1=-1.0, scalar2=1.0, op0=ALU.mult, op1=ALU.add), reads=["p64"], writes=["omm64"])
        S.op("dve", "tensor_scalar", dict(out=omm64[:, 5:6], in0=p64[:, 8:9], scalar1=-1.0, scalar2=1.0, op0=ALU.mult, op1=ALU.add), reads=["p64"], writes=["omm64"])
        S.op("dve", "tensor_scalar", dict(out=omm128[:, 0:1], in0=p128[:, 0:1], scalar1=-1.0, scalar2=1.0, op0=ALU.mult, op1=ALU.add), reads=["p128"], writes=["omm128"])
        S.op("dve", "tensor_scalar", dict(out=slc[:, :], in0=p128[:, 1:2], scalar1=0.8, scalar2=None, op0=ALU.mult), reads=["p128"], writes=["slc"])
        S.op("dve", "tensor_tensor", dict(out=lamt[:, 0, :], in0=lamt[:, 0, :], in1=lamt[:, 1, :], op=ALU.mult), reads=["lamt"], writes=["lamt"])
        S.op("dve", "tensor_tensor", dict(out=lamt[:, 2, :], in0=lamt[:, 2, :], in1=lamt[:, 3, :], op=ALU.mult), reads=["lamt"], writes=["lamt"])
        S.op("dve", "reduce_sum", dict(out=lams[:, 0:1], in_=lamt[:, 0, :], axis=AX.X), reads=["lamt"], writes=["lams"])
        S.op("dve", "reduce_sum", dict(out=lams[:, 1:2], in_=lamt[:, 2, :], axis=AX.X), reads=["lamt"], writes=["lams"])
        S.op("act", "activation", dict(out=lams[:, 2:4], in_=lams[:, 0:2], func=AF.Exp), reads=["lams"], writes=["lams"])
        S.op("dve", "tensor_tensor", dict(out=lams[:, 4:5], in0=lams[:, 3:4], in1=lams[:, 2:3], op=ALU.subtract), reads=["lams"], writes=["lams"])
        S.op("dve", "tensor_scalar", dict(out=lams[:, 5:6], in0=lams[:, 4:5], scalar1=-0.2, scalar2=None, op0=ALU.add), reads=["lams"], writes=["lams"])
        S.op("dve", "tensor_copy", dict(out=lamb[:, 0:1], in_=lams[:, 5:6]), reads=["lams"], writes=["lamb"])
        for k in range(8):
            st_ = stg[k % 2]
            src = st_[:, 0:2, :].rearrange("p a b -> p (a b)")[:, 0:NCOL]
            dma(src, wA[k * 128:(k + 1) * 128, :], ["stg0"])
            if k % 2:
                S.op("dve", "tensor_copy", dict(out=wAb[:, k, :], in_=src), reads=["stg0"], writes=["wAb"])
            else:
                S.op("act", "activation", dict(out=wAb[:, k, :], in_=src, func=AF.Copy), reads=["stg0"], writes=["wAb"])
        rkb = sb("rkb", [64, 2], BF16)
        S.op("dve", "tensor_copy", dict(out=rkb[:, 0:1], in_=p64[:, 9:10]), reads=["p64"], writes=["rkb"])
        S.op("dve", "tensor_copy", dict(out=rkb[:, 1:2], in_=p64[:, 9:10]), reads=["p64"], writes=["rkb"])

        groups = [("r", C_R, 64), ("k", C_K, 64), ("v", C_V, 64), ("wl", C_WL, 64), ("al", C_AL, 64), ("gl", C_GL, 128),
                  ("dq", C_DQ, 128), ("dk", C_DK, 128), ("dv", C_DV, 128)]
        pj_i = [0]
        rcur = [0]
        chunk_ctr = [0]

        def build_pre(bi):
            ops_ = []
            def OPp(eng, meth, kw, reads=(), writes=(), dma=False, skip_pe=False):
                ops_.append((eng, meth, kw, list(reads), list(writes), dma, skip_pe))
            def DMAp(out, in_, w, r=()):
                OPp("sp", "dma_start", dict(out=out, in_=in_), reads=list(r), writes=list(w), dma=True)
            if bi > 0:
                OPp("pool", "tensor_copy", dict(out=GL[bi % 2][:, 0:1], in_=GL[(bi - 1) % 2][:, BT // 64:BT // 64 + 1]), reads=["gLn_%d" % ((bi - 1) % 2)], writes=["gL_%d" % (bi % 2)])
            t0 = bi * BT
            t0 = bi * BT
            sg_ = stg[bi % 2]; sk = "stg0"
            if bi == 0:
                DMAp(sg_[:, :, :], xT[:, t0:t0 + BT].rearrange("(k p) t -> p k t", p=128), [sk])
            OPp("act", "activation", dict(out=sq[:, :, :], in_=sg_[:, :, :], func=AF.Square), reads=[sk], writes=["xn"])
            pn = pj[pj_i[0] % 2]; pnk = "m3"; pj_i[0] += 1
            for k in range(8):
                OPp("pe", "matmul", dict(out=pn[:, 0:BT], lhsT=ones_b[:, :], rhs=sq[:, k, :], start=(k == 0), stop=(k == 7)), reads=["xn", "ones_b"], writes=[pnk], skip_pe=(k > 0))
            OPp("act", "activation", dict(out=rstd_b[:, :], in_=pn[:, 0:BT], func=AF.Sqrt, scale=1.0 / D, bias=eps_t[:, 0:1]), reads=[pnk, "eps"], writes=["rstd_b"])
            OPp("dve", "reciprocal", dict(out=rstd_b[:, :], in_=rstd_b[:, :]), reads=["rstd_b"], writes=["rstd_b"])
            for k in range(8):
                OPp("dve", "scalar_tensor_tensor", dict(out=xn[:, k, :], in0=sg_[:, k, :], scalar=g1c[:, k:k + 1], in1=rstd_b[:, :], op0=ALU.mult, op1=ALU.mult),
                     reads=[sk, "g1c", "rstd_b"], writes=["xn"])
            if bi + 1 < NB:
                DMAp(sg_[:, :, :], xT[:, t0 + BT:t0 + 2 * BT].rearrange("(k p) t -> p k t", p=128), [sk])
            for gi, (gname, c0, M) in enumerate(groups):
                pp = pj[pj_i[0] % 2]; ppk = "m3"; pj_i[0] += 1
                for k in range(8):
                    OPp("pe", "matmul", dict(out=pp[0:M, 0:BT], lhsT=wAb[:, k, c0:c0 + M], rhs=xn[:, k, :], start=(k == 0), stop=(k == 7)), reads=["wAb", "xn"], writes=[ppk], skip_pe=(k > 0))
                if gi < 5:
                    OPp("act", "activation", dict(out=praw[:, gi, 1:BT + 1], in_=pp[0:64, 0:BT], func=AF.Identity), reads=[ppk], writes=["praw%d" % gi])
                    OPp("act", "activation", dict(out=tmp64[:, 0, :], in_=praw[:, gi, 0:BT], func=AF.Identity, scale=p64[:, gi:gi + 1]),
                         reads=["praw%d" % gi, "p64"], writes=["t0"])
                    OPp("dve", "scalar_tensor_tensor", dict(out=(xs[:, gi, :] if gi < 3 else tmp64b[:, gi - 2, :]), in0=praw[:, gi, 1:BT + 1], scalar=omm64[:, gi:gi + 1], in1=tmp64[:, 0, :], op0=ALU.mult, op1=ALU.add),
                         reads=["praw%d" % gi, "omm64", "t0"], writes=[("xs%d" % gi) if gi < 3 else ("kp", "bb")[gi - 3]])
                    OPp("pool", "tensor_copy", dict(out=praw[:, gi, 0:1], in_=praw[:, gi, BT:BT + 1]), reads=["praw%d" % gi], writes=["praw%d" % gi])
                elif gname == "gl":
                    OPp("act", "activation", dict(out=graw[:, 1:BT + 1], in_=pp[:, 0:BT], func=AF.Identity), reads=[ppk], writes=["graw"])
                    OPp("act", "activation", dict(out=Qf[:, :], in_=graw[:, 0:BT], func=AF.Identity, scale=p128[:, 0:1]), reads=["graw", "p128"], writes=["Qf"])
                    OPp("dve", "scalar_tensor_tensor", dict(out=Qf[:, :], in0=graw[:, 1:BT + 1], scalar=omm128[:, 0:1], in1=Qf[:, :], op0=ALU.mult, op1=ALU.add),
                         reads=["graw", "omm128", "Qf"], writes=["Qf"])
                    OPp("act", "activation", dict(out=GLS[bi % 2][:, :], in_=Qf[:, :], func=AF.Sigmoid), reads=["Qf"], writes=["gls_%d" % (bi % 2)])
                    OPp("pool", "tensor_copy", dict(out=graw[:, 0:1], in_=graw[:, BT:BT + 1]), reads=["graw"], writes=["graw"])
                elif gname == "dq":
                    OPp("act", "activation", dict(out=Qf[:, :], in_=pp[:, 0:BT], func=AF.Identity), reads=[ppk, "gls_%d" % (bi % 2)], writes=["Qf"])
                    qv = Qf[:, :].rearrange("p (a b l) -> p a b l", b=2, l=128)
                    gq = (bi // 2) % 2; i0 = 2 * (bi % 2)
                    OPp("dve", "tensor_scalar", dict(out=qtmp[:, :, :], in0=qv[:, :, 0, :], scalar1=psel[:, 0:1], scalar2=None, op0=ALU.mult), reads=["Qf", "psel"], writes=["qtmp"])
                    for mq in range(2):
                        rs_ = slice(64 * mq, 64 * mq + 64)
                        OPp("dve", "scalar_tensor_tensor", dict(out=Qs[mq][rs_, gq, i0 * 128:(i0 + 2) * 128].rearrange("p (a l) -> p a l", l=128), in0=qv[rs_, :, 1, :], scalar=psel[rs_, 1:2], in1=qtmp[rs_, :, :], op0=ALU.mult, op1=ALU.add),
                             reads=["Qf", "psel", "qtmp"], writes=["Qs%d" % mq])
                elif gname == "dk":
                    OPp("act", "activation", dict(out=KT[:, t0:t0 + BT], in_=pp[:, 0:BT], func=AF.Copy), reads=[ppk], writes=["KT"])
                else:
                    OPp("act", "activation", dict(out=DVB[bi % 2][:, :], in_=pp[:, 0:BT], func=AF.Copy), reads=[ppk], writes=["dvb_%d" % (bi % 2)])
            r_, k_, v_ = (xs[:, i, :] for i in range(3)); wl_ = tmp64b[:, 1, :]; al_ = tmp64b[:, 2, :]
            OPp("act", "activation", dict(out=bf64[:, 0, :], in_=wl_, func=AF.Tanh), reads=["kp"], writes=["bf0"])
            OPp("pool", "tensor_copy", dict(out=bf64[:, 1, :], in_=al_), reads=["bb"], writes=["bf1"])
            pw = pj[pj_i[0] % 2]; pwk = "m3"; pj_i[0] += 1
            OPp("pe", "matmul", dict(out=pw[0:64, 0:BT], lhsT=wdu[:, :], rhs=bf64[:, 0, :], start=True, stop=True), reads=["wsm", "bf0"], writes=[pwk])
            OPp("act", "activation", dict(out=tmp64[:, 2, :], in_=pw[0:64, 0:BT], func=AF.Sigmoid, bias=p64[:, 5:6]), reads=[pwk, "p64"], writes=["sg"])
            pa_ = pj[pj_i[0] % 2]; pak = "m3"; pj_i[0] += 1
            OPp("pe", "matmul", dict(out=pa_[0:64, 0:BT], lhsT=wiu[:, :], rhs=bf64[:, 1, :], start=True, stop=True), reads=["wsm", "bf1"], writes=[pak])
            OPp("act", "activation", dict(out=tmp64[:, 1, :], in_=pa_[0:64, 0:BT], func=AF.Sigmoid, bias=p64[:, 6:7]), reads=[pak, "p64"], writes=["a"])
            OPp("dve", "tensor_scalar", dict(out=tmp64[:, 3, :], in0=k_, scalar1=p64[:, 7:8], scalar2=None, op0=ALU.mult), reads=["xs1", "p64"], writes=["kk"])
            OPp("act", "activation", dict(out=bf64[:, 2, :], in_=tmp64[:, 3, :], func=AF.Square), reads=["kk"], writes=["bf2"])
            pk_ = pj[pj_i[0] % 2]; pkk = "m3"; pj_i[0] += 1
            OPp("pe", "matmul", dict(out=pk_[0:64, 0:BT], lhsT=ones_b[0:64, 0:64], rhs=bf64[:, 2, :], start=True, stop=True), reads=["ones_b", "bf2"], writes=[pkk])
            OPp("act", "activation", dict(out=tmp64[:, 0, :], in_=pk_[0:64, 0:BT], func=AF.Sqrt, bias=eps_t[0:64, 3:4]), reads=[pkk, "eps"], writes=["t0"])
            OPp("dve", "reciprocal", dict(out=tmp64[:, 0, :], in_=tmp64[:, 0, :]), reads=["t0"], writes=["t0"])
            OPp("dve", "tensor_tensor", dict(out=tmp64[:, 3, :], in0=tmp64[:, 3, :], in1=tmp64[:, 0, :], op=ALU.mult), reads=["kk", "t0"], writes=["kk"])
            OPp("dve", "tensor_scalar", dict(out=tmp64b[:, 1, :], in0=tmp64[:, 1, :], scalar1=p64[:, 8:9], scalar2=omm64[:, 5:6], op0=ALU.mult, op1=ALU.add),
                 reads=["a", "p64", "omm64"], writes=["kp"])
            OPp("pool", "tensor_tensor", dict(out=tmp64b[:, 1, :], in0=tmp64b[:, 1, :], in1=k_, op=ALU.mult), reads=["kp", "xs1"], writes=["kp"])
            OPp("pool", "tensor_tensor", dict(out=tmp64b[:, 2, :], in0=tmp64[:, 3, :], in1=tmp64[:, 1, :], op=ALU.mult), reads=["kk", "a"], writes=["bb"])
            OPp("dve", "tensor_tensor_scan", dict(out=tmp64[:, 0, :], data0=rmask[:, :], data1=tmp64[:, 2, :], initial=0.0, op0=ALU.mult, op1=ALU.add),
                 reads=["rmask", "sg"], writes=["t0"])
            OPp("act", "activation", dict(out=tmp64[:, 4, :], in_=tmp64[:, 0, :], func=AF.Exp, scale=CDEC), reads=["t0"], writes=["epos"])
            OPp("act", "activation", dict(out=tmp64[:, 5, :], in_=tmp64[:, 0, :], func=AF.Exp, scale=-CDEC), reads=["t0"], writes=["eneg"])
            OPp("pool", "tensor_tensor", dict(out=tmp64[:, 0, :], in0=tmp64[:, 0, :], in1=tmp64[:, 2, :], op=ALU.subtract), reads=["t0", "sg"], writes=["t0"])
            OPp("act", "activation", dict(out=tmp64b[:, 0, :], in_=tmp64[:, 0, :], func=AF.Exp, scale=-CDEC), reads=["t0"], writes=["eprev"])
            OPp("pool", "tensor_copy", dict(out=GL[bi % 2][:, 1:BT // 64 + 1], in_=tmp64[:, 5, :].rearrange("p (c l) -> p c l", l=64)[:, :, 63]), reads=["eneg"], writes=["gLn_%d" % (bi % 2)])
            fm = lambda q: FM[bi % 2][:, :, q, :]
            v3 = lambda ap: ap.rearrange("p (t l) -> p t l", l=128)
            OPp("dve", "scalar_tensor_tensor", dict(out=fm(0), in0=v3(tmp64[:, 3, :]), scalar=-1.0, in1=v3(tmp64b[:, 0, :]), op0=ALU.mult, op1=ALU.mult),
                 reads=["kk", "eprev"], writes=["fm0_%d" % (bi % 2)])
            OPp("pool", "tensor_tensor", dict(out=fm(1), in0=v3(r_), in1=v3(tmp64[:, 5, :]), op=ALU.mult), reads=["xs0", "eneg"], writes=["fm1_%d" % (bi % 2)])
            OPp("dve", "tensor_tensor", dict(out=fm(2), in0=v3(tmp64b[:, 2, :]), in1=v3(tmp64[:, 4, :]), op=ALU.mult), reads=["bb", "epos"], writes=["fm2_%d" % (bi % 2)])
            OPp("pool", "tensor_tensor", dict(out=fm(3), in0=v3(tmp64b[:, 1, :]), in1=v3(tmp64[:, 4, :]), op=ALU.mult), reads=["kp", "epos"], writes=["fm3_%d" % (bi % 2)])
            OPp("act", "activation", dict(out=fm(4), in_=v3(v_), func=AF.Copy), reads=["xs2"], writes=["fm4_%d" % (bi % 2)])
            OPp("dve", "tensor_tensor", dict(out=RK[bi % 2][:, :], in0=r_, in1=tmp64b[:, 1, :], op=ALU.mult), reads=["xs0", "kp"], writes=["rk_%d" % (bi % 2)])

            return ops_

        pending_fin = []
        pre0 = build_pre(0)
        for o2 in pre0:
            S.op(o2[0], o2[1], o2[2], reads=o2[3], writes=o2[4], dma=o2[5], skip_pe=o2[6])
        for bi in range(NB):
            t0 = bi * BT
            nxt_pre = build_pre(bi + 1) if bi + 1 < NB else []
            if stop == 'all' and bi + 1 < NB or (stop == 'all' and bi % 2 == 0):
                nxt_pre = pending_fin + nxt_pre
                pending_fin = []
            if stop == 'pre':
                for o2 in nxt_pre:
                    S.op(o2[0], o2[1], o2[2], reads=o2[3], writes=o2[4], dma=o2[5], skip_pe=o2[6])
            def tile_block(ti, th):
                pre_, st_ = [], []
                cur = [pre_]
                def km(k):
                    return (k + "@%d" % th) if k in LOCALKEYS else k
                def OP(eng, meth, kw, reads=(), writes=(), dma=False, skip_pe=False):
                    cur[0].append((eng, meth, kw, [km(r) for r in reads], [km(w) for w in writes], dma, skip_pe))
                def DMA(out, in_, w, r=()):
                    OP("sp", "dma_start", dict(out=out, in_=in_), reads=list(r), writes=list(w), dma=True)
                Msb, tmops, Zf, Zb, XY, Lc, Qc, Gz, Yl, gtm, rkk, yb, stt = TB[th]
                pMa, pMb, pZ, pX, pT, pPQ, pG, pY, pY2, pSt, pgt, prk = TP[th]
                tg = bi * TPB + ti
                tl = slice(ti * 128, (ti + 1) * 128)
                A_T, R_T, B_T, K_T, V_T = (FM[bi % 2][:, ti, q, :] for q in range(5))
                AR = FM[bi % 2][:, ti, 0:2, :].rearrange("p a b -> p (a b)")
                OP("pe", "transpose", dict(out=pT[:, 0:2, :].rearrange("p a b -> p (a b)"), in_=DVB[bi % 2][:, tl], identity=ident_b[:, :]), reads=["dvb_%d" % (bi % 2), "ident_b"], writes=["pT"])
                OP("act", "activation", dict(out=Vtm[:, tg, :], in_=pT[:, 0:2, :].rearrange("p a b -> p (a b)"), func=AF.Copy), reads=["pT"], writes=["Vtm"])
                OP("pe", "matmul", dict(out=pMa[:, 0:2, :].rearrange("p a b -> p (a b)"), lhsT=B_T, rhs=AR, start=True, stop=True), reads=["fm0_%d" % (bi % 2), "fm1_%d" % (bi % 2), "fm2_%d" % (bi % 2)], writes=["pMa"])
                OP("pe", "matmul", dict(out=pMa[:, 2:4, :].rearrange("p a b -> p (a b)"), lhsT=K_T, rhs=AR, start=True, stop=True), reads=["fm0_%d" % (bi % 2), "fm1_%d" % (bi % 2), "fm3_%d" % (bi % 2)], writes=["pMa"])
                OP("pe", "matmul", dict(out=pMb[:, :], lhsT=A_T, rhs=B_T, start=True, stop=True), reads=["fm0_%d" % (bi % 2), "fm2_%d" % (bi % 2)], writes=["pMb"])
                OP("dve", "tensor_tensor", dict(out=Msb[:, 0:4, :], in0=pMa[:, :, :], in1=masks[:, 0:4, :], op=ALU.mult), reads=["pMa", "masks"], writes=["Msb"])
                OP("dve", "tensor_tensor", dict(out=Msb[:, 4, :], in0=pMb[:, :], in1=masks[:, 4, :], op=ALU.mult), reads=["pMb", "masks"], writes=["Msb4"])
                for qi_, src in enumerate((A_T, B_T, K_T, V_T)):
                    OP("pe", "transpose", dict(out=pT[:, qi_, :], in_=src, identity=ident_b[0:64, 0:64]), reads=["fm0_%d" % (bi % 2), "fm2_%d" % (bi % 2), "fm3_%d" % (bi % 2), "fm4_%d" % (bi % 2), "ident_b"], writes=["pT"])
                OP("act", "activation", dict(out=tmops[:, :, :], in_=pT[:, :, :], func=AF.Copy), reads=["pT"], writes=["tmops"])
                A_tm, B_tm, K_tm, V_tm = (tmops[:, q, :] for q in range(4))
                OP("pe", "matmul", dict(out=pZ[:, 64:128], lhsT=Msb[:, 2, :], rhs=V_tm, start=True, stop=True), reads=["Msb", "tmops"], writes=["pZ"])
                OP("dve", "tensor_copy", dict(out=Zf[:, 0:64], in_=A_tm), reads=["tmops"], writes=["Zf"])
                OP("dve", "tensor_copy", dict(out=Zf[:, 64:128], in_=pZ[:, 64:128]), reads=["pZ"], writes=["Zf"])
                OP("act", "activation", dict(out=Zb[:, :], in_=Zf[:, :], func=AF.Copy), reads=["Zf"], writes=["Zb"])
                X = Msb[:, 0, :]; Yp = Msb[:, 4, :]; xk = ["Msb", "Msb4"]
                for lv in range(6):
                    OP("pe", "matmul", dict(out=pZ[:, :], lhsT=X, rhs=Zb[:, :], start=True, stop=True), reads=xk + ["Zb"], writes=["pZ"])
                    OP("dve", "tensor_tensor", dict(out=Zf[:, :], in0=Zf[:, :], in1=pZ[:, :], op=ALU.add), reads=["Zf", "pZ"], writes=["Zf"])
                    OP("act", "activation", dict(out=Zb[:, :], in_=Zf[:, :], func=AF.Copy), reads=["Zf"], writes=["Zb"])
                    if lv < 5:
                        OP("pe", "matmul", dict(out=pX[:, 0, :], lhsT=Yp, rhs=X, start=True, stop=True), reads=xk, writes=["pX"])
                        OP("pe", "matmul", dict(out=pX[:, 1, :], lhsT=X, rhs=Yp, start=True, stop=True), reads=xk, writes=["pX"])
                        nxt = XY[lv % 2]
                        OP("pool" if False else "dve", "tensor_copy", dict(out=nxt[:, :, :], in_=pX[:, :, :]), reads=["pX"], writes=["XY%d" % (lv % 2)])
                        X = nxt[:, 0, :]; Yp = nxt[:, 1, :]; xk = ["XY%d" % (lv % 2)]
                Ah = Zb[:, 0:64]; U = Zb[:, 64:128]
                for c in range(2):
                    cs_ = slice(c * 64, (c + 1) * 64)
                    OP("pe", "matmul", dict(out=pPQ[:, c, :], lhsT=Zb[cs_, 0:64], rhs=tmops[cs_, 1, :], start=True, stop=False), reads=["Zb", "tmops"], writes=["pPQ"])
                    OP("pe", "matmul", dict(out=pPQ[:, c, :], lhsT=ident_b[0:64, 0:64], rhs=ident_b[0:64, 0:64], start=False, stop=True), reads=["ident_b"], writes=["pPQ"])
                    OP("pe", "matmul", dict(out=pPQ[:, 2 + c, :], lhsT=tmops[cs_, 1, :], rhs=Zb[cs_, 64:128], start=True, stop=False), reads=["Zb", "tmops"], writes=["pPQ"])
                    OP("pe", "matmul", dict(out=pPQ[:, 2 + c, :], lhsT=tmops[cs_, 2, :], rhs=tmops[cs_, 3, :], start=False, stop=True), reads=["tmops"], writes=["pPQ"])
                gcol = lambda c: GL[bi % 2][:, 2 * ti + c:2 * ti + c + 1]
                for c in range(2):
                    OP("act", "activation", dict(out=Lc[:, c, :], in_=pPQ[:, c, :], func=AF.Identity, scale=gcol(c)), reads=["pPQ", "gL_%d" % (bi % 2), "gLn_%d" % (bi % 2)], writes=["Lc"])
                OP("dve", "tensor_copy", dict(out=Qc[:, :, :], in_=pPQ[:, 2:4, :]), reads=["pPQ"], writes=["Qc"])
                OP("pe", "matmul", dict(out=pG[:, :], lhsT=Zb[:, 0:64], rhs=Msb[:, 1, :], start=True, stop=True), reads=["Zb", "Msb"], writes=["pG"])
                for c in range(2):
                    cs_ = slice(c * 64, (c + 1) * 64)
                    OP("dve", "tensor_tensor", dict(out=Gz[:, c, cs_], in0=pG[:, cs_], in1=R_T[:, cs_], op=ALU.add), reads=["pG", "fm1_%d" % (bi % 2)], writes=["Gz"])
                    OP("dve", "tensor_scalar", dict(out=Gz[:, c, cs_], in0=Gz[:, c, cs_], scalar1=gcol(c), scalar2=None, op0=ALU.mult), reads=["Gz", "gL_%d" % (bi % 2), "gLn_%d" % (bi % 2)], writes=["Gz"])
                OP("pe", "matmul", dict(out=pY[:, :], lhsT=Msb[:, 1, :], rhs=Zb[:, 64:128], start=True, stop=False), reads=["Msb", "Zb"], writes=["pY"])
                OP("pe", "matmul", dict(out=pY[:, :], lhsT=Msb[:, 3, :], rhs=V_tm, start=False, stop=True), reads=["Msb", "tmops"], writes=["pY"])
                OP("act", "activation", dict(out=Yl[:, :], in_=pY[:, :], func=AF.Identity), reads=["pY"], writes=["Yl"])
                OP("pe", "matmul", dict(out=pgt[:, :], lhsT=GLS[bi % 2][:, tl], rhs=wgu[:, :], start=True, stop=True), reads=["gls_%d" % (bi % 2), "wsm"], writes=["pgt"])
                OP("act", "activation", dict(out=gtm[:, :], in_=pgt[:, :], func=AF.Identity), reads=["pgt"], writes=["gtm"])
                OP("pe", "matmul", dict(out=prk[:, 2:4], lhsT=RK[bi % 2][:, tl], rhs=rkb[:, :], start=True, stop=True), reads=["rk_%d" % (bi % 2), "rkb"], writes=["prk"])
                OP("dve", "tensor_copy", dict(out=rkk[:, :], in_=prk[:, 2:3]), reads=["prk"], writes=["rkk"])
                cur[0] = st_
                Ra = Rst[rcur[0] % 3]; Rb = Rst[(rcur[0] + 1) % 3]; Rc2 = Rst[(rcur[0] + 2) % 3]
                ka, kb_, kc_ = "R%d" % (rcur[0] % 3), "R%d" % ((rcur[0] + 1) % 3), "R%d" % ((rcur[0] + 2) % 3)
                OP("pe", "matmul", dict(out=pSt[:, :], lhsT=Lc[:, 0, :], rhs=Ra[:, :], start=True, stop=True), reads=["Lc", ka], writes=["pSt"])
                OP("dve", "tensor_tensor", dict(out=Rb[:, :], in0=pSt[:, :], in1=Qc[:, 0, :], op=ALU.add), reads=["pSt", "Qc"], writes=[kb_])
                OP("pe", "matmul", dict(out=pSt[:, :], lhsT=Lc[:, 1, :], rhs=Rb[:, :], start=True, stop=True), reads=["Lc", kb_], writes=["pSt"])
                OP("dve", "tensor_tensor", dict(out=Rc2[:, :], in0=pSt[:, :], in1=Qc[:, 1, :], op=ALU.add), reads=["pSt", "Qc"], writes=[kc_])
                OP("pe", "matmul", dict(out=pY2[:, :], lhsT=Gz[:, 0, :], rhs=Ra[:, :], start=True, stop=False), reads=["Gz", ka], writes=["pY2"])
                OP("pe", "matmul", dict(out=pY2[:, :], lhsT=Gz[:, 1, :], rhs=Rb[:, :], start=False, stop=True), reads=["Gz", kb_], writes=["pY2"])
                rcur[0] += 2
                y0 = yb[:, 0, :]; y1 = yb[:, 1, :]
                OP("dve", "tensor_tensor", dict(out=y0, in0=pY2[:, :], in1=Yl[:, :], op=ALU.add), reads=["pY2", "Yl"], writes=["y0"])
                OP("dve", "scalar_tensor_tensor", dict(out=y0, in0=V_tm, scalar=rkk[:, 0:1], in1=y0, op0=ALU.mult, op1=ALU.add), reads=["tmops", "rkk", "y0"], writes=["y0"])
                OP("act", "activation", dict(out=y1, in_=y0, func=AF.Identity, accum_out=stt[:, 0:1]), reads=["y0"], writes=["y1", "stt0"])
                OP("dve", "tensor_scalar", dict(out=stt[:, 1:2], in0=stt[:, 0:1], scalar1=-1.0 / 64, scalar2=None, op0=ALU.mult), reads=["stt0"], writes=["stt1"])
                OP("act", "activation", dict(out=y1, in_=y0, func=AF.Square, bias=stt[:, 1:2], accum_out=stt[:, 2:3]), reads=["y0", "stt1"], writes=["y1", "stt2"])
                OP("act", "activation", dict(out=stt[:, 3:4], in_=stt[:, 2:3], func=AF.Sqrt, scale=1.0 / 64, bias=eps_t[:, 1:2]), reads=["stt2", "eps"], writes=["stt3"])
                OP("dve", "reciprocal", dict(out=stt[:, 3:4], in_=stt[:, 3:4]), reads=["stt3"], writes=["stt3"])
                OP("dve", "tensor_scalar", dict(out=y1, in0=y0, scalar1=stt[:, 1:2], scalar2=stt[:, 3:4], op0=ALU.add, op1=ALU.mult), reads=["y0", "stt1", "stt3"], writes=["y1"])
                OP("pool", "tensor_tensor", dict(out=y1, in0=y1, in1=b64[:, 0, :], op=ALU.mult), reads=["y1", "b64"], writes=["y1"])
                OP("pool", "tensor_tensor", dict(out=y1, in0=y1, in1=b64[:, 1, :], op=ALU.add), reads=["y1", "b64"], writes=["y1"])
                OP("pool", "tensor_tensor", dict(out=y1, in0=y1, in1=gtm[:, :], op=ALU.mult), reads=["y1", "gtm"], writes=["y1"])
                DMA(y_out[tg * 128:(tg + 1) * 128, :], y1, [], r=["y1"])

                return pre_, st_
            for ti0 in range(0, TPB if stop != 'pre' else 0, 2):
                preA, stA = tile_block(ti0, 0)
                preB, stB = tile_block(ti0 + 1, 1)
                merged = []
                for ii in range(max(len(preA), len(preB))):
                    if ii < len(preA):
                        merged.append(preA[ii])
                    if ii < len(preB):
                        merged.append(preB[ii])
                tile_ops_ = merged + stA + stB
                npairs = max(1, TPB // 2)
                pi_ = ti0 // 2
                nx = nxt_pre[(len(nxt_pre) * pi_) // npairs:(len(nxt_pre) * (pi_ + 1)) // npairs]
                ia = 0; ib = 0
                ratio = (len(nx) + 1e-9) / max(1, len(tile_ops_))
                acc = 0.0
                for opx in tile_ops_:
                    S.op(opx[0], opx[1], opx[2], reads=opx[3], writes=opx[4], dma=opx[5], skip_pe=opx[6])
                    acc += ratio
                    while acc >= 1.0 and ib < len(nx):
                        o2 = nx[ib]; ib += 1; acc -= 1.0
                        S.op(o2[0], o2[1], o2[2], reads=o2[3], writes=o2[4], dma=o2[5], skip_pe=o2[6])
                while ib < len(nx):
                    o2 = nx[ib]; ib += 1
                    S.op(o2[0], o2[1], o2[2], reads=o2[3], writes=o2[4], dma=o2[5], skip_pe=o2[6])
            if bi % 2 == 1 and stop == 'all':
                g = bi // 2; gq = g % 2
                S.op("dve", "memset", dict(ap=bar[:, 0:1], constant=0.0), writes=["xn", "bar0", "rstd_b", "Qf"])
                S.op("pool", "memset", dict(ap=bar[:, 1:2], constant=0.0), reads=["xn"], writes=["bar1"])
                S.op("act", "activation", dict(out=bar[:, 2:3], in_=bar[:, 0:1], func=AF.Copy), reads=["xn", "bar0"], writes=["bar2"])
                nkb = 8 * g + 8
                Sbank = [[(pSa_t, "pSa"), (pMa.rearrange("p a b -> p (a b)"), "pMa")], [(m1, "m1"), (m2, "m2")]]
                O_b = [(pO0_t, "pO0"), (pO1_t, "pO1")]
                Lb = m3f
                def qk_exp(j):
                    par = j % 2
                    for mp in range(2):
                        sbk, skey = Sbank[par][mp]
                        S.op("pe", "matmul", dict(out=sbk[:, :], lhsT=KT[:, j * 128:(j + 1) * 128], rhs=Qs[mp][:, gq, :], start=True, stop=False),
                             reads=["KT", "Qs%d" % mp], writes=[skey], skip_pe=True)
                        S.op("pe", "matmul", dict(out=sbk[:, :], lhsT=ones_b[:, :], rhs=qaug_pad[:, :], start=False, stop=True),
                             reads=["ones_b", "qaug_pad"], writes=[skey], skip_pe=True)
                    far = j < 8 * g
                    n = 8 * g - j + 7
                    for mp in range(2):
                        sbk, skey = Sbank[par][mp]
                        pk2 = "Pt%d%d" % (par, mp)
                        if far:
                            S.op("act", "activation", dict(out=Pt[:, par, mp, :], in_=sbk[:, :], func=AF.Exp, scale=0.125, bias=abias[:, n:n + 1]), reads=[skey, "abias"], writes=[pk2])
                        else:
                            jrel = j - 8 * g
                            S.op("dve", "scalar_tensor_tensor", dict(out=tn[:, mp, :], in0=sbk[:, :], scalar=0.125, in1=maskm[:, jrel, :], op0=ALU.mult, op1=ALU.add),
                                 reads=[skey, "maskm"], writes=["tn%d" % mp])
                            S.op("act", "activation", dict(out=Pt[:, par, mp, :], in_=tn[:, mp, :], func=AF.Exp, bias=abias[:, n:n + 1]), reads=["tn%d" % mp, "abias"], writes=[pk2])

                def pv_l(j):
                    par = j % 2
                    for mp in range(2):
                        pk2 = "Pt%d%d" % (par, mp)
                        S.op("pe", "matmul", dict(out=O_b[mp][0][:, :], lhsT=Vtm[:, j, :], rhs=Pt[:, par, mp, :], start=(j == 0), stop=(j == nkb - 1)), reads=[pk2, "Vtm"], writes=[O_b[mp][1]], skip_pe=True)
                    for mp in range(2):
                        pk2 = "Pt%d%d" % (par, mp)
                        if j < 8 * g:
                            if j == 0:
                                S.op("dve", "tensor_copy", dict(out=Lacc[mp], in_=Pt[:, par, mp, :]), reads=[pk2], writes=["Lacc%d" % mp])
                            else:
                                S.op("dve", "tensor_tensor", dict(out=Lacc[mp], in0=Lacc[mp], in1=Pt[:, par, mp, :], op=ALU.add), reads=[pk2, "Lacc%d" % mp], writes=["Lacc%d" % mp])
                        else:
                            S.op("pe", "matmul", dict(out=Lb[:, :], lhsT=esel[:, mp, :], rhs=Pt[:, par, mp, :], start=(j == 8 * g and mp == 0), stop=(g == 0 and j == nkb - 1 and mp == 1)), reads=[pk2, "esel"], writes=["m3"], skip_pe=True)

                qk_exp(0)
                for j in range(nkb):
                    if j + 1 < nkb:
                        qk_exp(j + 1)
                    pv_l(j)
                if g > 0:
                    for mp in range(2):
                        S.op("act", "activation", dict(out=Lhl[:, mp, 0, :], in_=Lacc[mp], func=AF.Copy), reads=["Lacc%d" % mp], writes=["Pt%d0" % mp, "Pt%d1" % mp])
                        S.op("dve", "tensor_tensor", dict(out=Lacc[mp], in0=Lacc[mp], in1=Lhl[:, mp, 0, :], op=ALU.subtract), reads=["Lacc%d" % mp, "Pt%d0" % mp, "Pt%d1" % mp], writes=["Lacc%d" % mp])
                        S.op("act", "activation", dict(out=Lhl[:, mp, 1, :], in_=Lacc[mp], func=AF.Copy), reads=["Lacc%d" % mp], writes=["Pt%d0" % mp, "Pt%d1" % mp])
                    for mp in range(2):
                        for hl in range(2):
                            S.op("pe", "matmul", dict(out=Lb[:, :], lhsT=esel[:, mp, :], rhs=Lhl[:, mp, hl, :], start=False, stop=(mp == 1 and hl == 1)), reads=["Pt%d0" % mp, "Pt%d1" % mp, "esel"], writes=["m3"], skip_pe=True)
                S.op("dve", "reciprocal", dict(out=rls[:, :], in_=Lb[0:2, :]), reads=["m3"], writes=["rls"])
                dma(rl_d[:, :], rls[:, :], ["rl_d"], r=["rls"])
                for mp in range(2):
                    dma(rb[:, mp, :], rl_d[mp:mp + 1, :].partition_broadcast(128), ["rb%d" % mp], r=["rl_d"])
                S.op("act", "activation", dict(out=oAB[:, 0, :], in_=pO0_t[:, :], func=AF.Copy), reads=["pO0"], writes=["oA"])
                S.op("dve", "tensor_copy", dict(out=oAB[:, 1, :], in_=pO1_t[:, :]), reads=["pO1"], writes=["oB"])
                FIN = []
                FIN.append(("dve", "tensor_tensor", dict(out=oAB[:, 0, :], in0=oAB[:, 0, :], in1=rb[:, 0, :], op=ALU.mult), ["oA", "rb0"], ["oA"], False, False))
                FIN.append(("dve", "scalar_tensor_tensor", dict(out=oAB[:, 1, :], in0=oAB[:, 1, :], scalar=lamb[:, 0:1], in1=rb[:, 1, :], op0=ALU.mult, op1=ALU.mult), ["oB", "rb1", "lamb"], ["oB"], False, False))
                FIN.append(("pool", "tensor_tensor", dict(out=oAB[:, 0, :], in0=oAB[:, 0, :], in1=oAB[:, 1, :], op=ALU.add), ["oA", "oB"], ["oA"], False, False))
                FIN.append(("act", "activation", dict(out=sqo[:, :], in_=oAB[:, 0, :], func=AF.Square), ["oA"], ["Pt00"], False, False))
                FIN.append(("pe", "matmul", dict(out=pSa_t[:, :], lhsT=ones_b[:, :], rhs=sqo[:, :], start=True, stop=True), ["Pt00", "ones_b"], ["pSa"], False, False))
                FIN.append(("act", "activation", dict(out=oAB[:, 1, :], in_=pSa_t[:, :], func=AF.Sqrt, scale=1.0 / 128, bias=eps_t[:, 2:3]), ["pSa", "eps"], ["oB"], False, False))
                FIN.append(("dve", "reciprocal", dict(out=oAB[:, 1, :], in_=oAB[:, 1, :]), ["oB"], ["oB"], False, False))
                FIN.append(("dve", "scalar_tensor_tensor", dict(out=oAB[:, 1, :], in0=oAB[:, 0, :], scalar=slc[:, 0:1], in1=oAB[:, 1, :], op0=ALU.mult, op1=ALU.mult), ["oA", "oB", "slc"], ["oB"], False, False))
                FIN.append(("sp", "dma_start", dict(out=o_out[:, g * 512:(g + 1) * 512], in_=oAB[:, 1, :]), ["oB"], ["oB"], True, False))
                FIN.append(("act", "activation", dict(out=bar[:, 3:4], in_=bar[:, 2:3], func=AF.Copy), ["tn0", "tn1", "oA", "oB", "bar2", "Lacc0", "Lacc1"], ["xn", "rstd_b", "Qf"], False, False))
                pending_fin = FIN
        for o2 in pending_fin:
            S.op(o2[0], o2[1], o2[2], reads=o2[3], writes=o2[4], dma=o2[5], skip_pe=o2[6])
        S.emit()
        pass
    return nc


D = 1024
DFF = 2816
NFF = 22
EPS = 1e-6


def build_B(ntok_halo=128, ntok=2048, BW=256):
    NT = ntok_halo + ntok
    nc = bass.Bass("TRN2", target_bir_lowering=False)
    mixT = nc.dram_tensor("mixT", [D, NT], F32, kind="ExternalInput").ap()
    xr = nc.dram_tensor("xr", [NT, D], F32, kind="ExternalInput").ap()
    w_out = nc.dram_tensor("w_out", [D, D], F32, kind="ExternalInput").ap()
    w_up = nc.dram_tensor("w_up", [D, 2 * DFF], F32, kind="ExternalInput").ap()
    w_down = nc.dram_tensor("w_down", [DFF, D], F32, kind="ExternalInput").ap()
    gbd = nc.dram_tensor("gb", [128, 2, D], F32, kind="ExternalInput").ap()
    g2d = nc.dram_tensor("g2c", [128, 8], F32, kind="ExternalInput").ap()
    cpd = nc.dram_tensor("convp", [128, 44, 4], F32, kind="ExternalInput").ap()
    outd = nc.dram_tensor("out", [ntok, D], F32, kind="ExternalOutput").ap()

    with ExitStack() as es:
        S = Sched(nc, es)
        S.pe_skip_all = True
        sb = lambda name, shape, dt: es.enter_context(nc.sbuf_tensor(name, shape, dt))
        ps = lambda name, shape, dt: es.enter_context(nc.psum_tensor(name, shape, dt))
        wup = sb("wup", [128, 8, 2 * DFF], BF16)
        wdn = sb("wdn", [128, NFF, D], BF16)
        wout = sb("wout", [128, 8, D], BF16)
        stg = [sb("stg%d" % i, [128, 1024], F32) for i in range(2)]
        mixb = sb("mixb", [128, 8, BW], BF16)
        h = sb("h", [128, BW // 128, D], F32)
        hn = sb("hn", [128, BW // 128, D], BF16)
        hnT = sb("hnT", [128, 8, BW], BF16)
        actT = sb("actT", [128, NFF, BW], BF16)
        ftmp = sb("ftmp", [128, 2, 4 * BW + 4], F32)
        gb = sb("gb_s", [128, 2, D], F32)
        g2c = sb("g2c_s", [128, 8], F32)
        cp = sb("cp", [128, 44, 4], F32)
        carry = sb("carry", [128, 44, 2], F32)
        ident = sb("ident", [128, 128], BF16)
        st = sb("st", [128, 16], F32)
        pA = [ps("pA%d" % i, [128, D], F32) for i in range(2)]
        pu = [ps("pu%d" % i, [128, 2, 256], F32) for i in range(2)]
        pt = ps("pt", [128, 8, 128], BF16)

        stg_i = [0]

        def load_convert(dst_ap, src_ap, shape_free, key, eng_rr=[0]):
            i = stg_i[0] % 2
            stg_i[0] += 1
            n = int(np.prod(shape_free))
            sv = stg[i][:, 0:n]
            if len(shape_free) == 2:
                sv = sv.rearrange("p (a b) -> p a b", b=shape_free[1])
            S.op("sp", "dma_start", dict(out=sv, in_=src_ap), writes=["stg%d" % i], dma=True)
            engs = ["act", "dve"]
            en = engs[eng_rr[0] % 2]
            eng_rr[0] += 1
            if en == "act":
                S.op("act", "activation", dict(out=dst_ap, in_=sv, func=AF.Copy), reads=["stg%d" % i], writes=[key])
            else:
                S.op(en, "tensor_copy", dict(out=dst_ap, in_=sv), reads=["stg%d" % i], writes=[key])

        S.op("sp", "dma_start", dict(out=gb[:, :, :], in_=gbd[:, :, :]), writes=["gb"], dma=True)
        S.op("sp", "dma_start", dict(out=g2c[:, :], in_=g2d[:, :]), writes=["g2c"], dma=True)
        S.op("sp", "dma_start", dict(out=cp[:, :, :], in_=cpd[:, :, :]), writes=["cp"], dma=True)
        S.op("pool", "memset", dict(ap=carry[:, :, :], constant=0.0), writes=["carry"])
        S.op("pool", "memset", dict(ap=ident[:, :], constant=1.0), writes=["ident"])
        S.op("pool", "affine_select", dict(out=ident[:, :], in_=ident[:, :], pattern=[[-1, 128]],
                                               compare_op=ALU.is_equal, fill=0.0, base=0, channel_multiplier=1),
             reads=["ident"], writes=["ident"])
        for k in range(8):
            load_convert(wout[:, k, :], w_out[k * 128:(k + 1) * 128, :], [1024], "wout")
        for k in range(8):
            for c in range(0, 2 * DFF, 1024):
                w = min(1024, 2 * DFF - c)
                load_convert(wup[:, k, c:c + w], w_up[k * 128:(k + 1) * 128, c:c + w], [w], "wup")
        for j in range(NFF):
            load_convert(wdn[:, j, :], w_down[j * 128:(j + 1) * 128, :], [1024], "wdn")

        def rstd_from(ss_ap, n_el, eps, key_in, key_out, out_ap):
            S.op("act", "activation", dict(out=out_ap, in_=ss_ap, func=AF.Sqrt, scale=1.0 / n_el, bias=eps_t[:, 0:1]),
                 reads=[key_in, "eps"], writes=[key_out])
            S.op("dve", "reciprocal", dict(out=out_ap, in_=out_ap), reads=[key_out], writes=[key_out])

        eps_t = sb("eps_t", [128, 1], F32)
        S.op("pool", "memset", dict(ap=eps_t[:, :], constant=EPS), writes=["eps"])

        batches = []
        t0 = 0
        if ntok_halo:
            batches.append((0, ntok_halo, True))
            t0 = ntok_halo
        while t0 < NT:
            batches.append((t0, min(BW, NT - t0), False))
            t0 += BW
        out_i = [0]
        for (t0, W, is_halo) in batches:
            nt = W // 128
            for hf in range(2):
                load_convert(mixb[:, hf * 4:(hf + 1) * 4, 0:W],
                             mixT[hf * 512:(hf + 1) * 512, t0:t0 + W].rearrange("(k p) t -> p k t", p=128),
                             [4, W], "mixb")
            S.op("sp", "dma_start", dict(out=h[:, 0:nt, :], in_=xr[t0:t0 + W, :].rearrange("(j p) d -> p j d", p=128)),
                 writes=["h"], dma=True)
            for j in range(nt):
                pa = pA[j % 2]; pk = "pA%d" % (j % 2)
                for hf in range(2):
                    for k in range(8):
                        S.op("pe", "matmul", dict(out=pa[:, hf * 512:(hf + 1) * 512], lhsT=mixb[:, k, j * 128:(j + 1) * 128],
                                                            rhs=wout[:, k, hf * 512:(hf + 1) * 512], start=(k == 0), stop=(k == 7)),
                             reads=["mixb", "wout"], writes=[pk])
                ti = stg_i[0] % 2; stg_i[0] += 1
                tmp = stg[ti][:, :]; tk = "stg%d" % ti
                junk = tmp[:, 0:512]
                for hf in range(2):
                    S.op("act", "activation", dict(out=junk, in_=pa[:, hf * 512:(hf + 1) * 512], func=AF.Square, accum_out=st[:, hf:hf + 1]),
                         reads=[pk], writes=[tk, "st%d" % hf])
                S.op("dve", "tensor_tensor", dict(out=st[:, 2:3], in0=st[:, 0:1], in1=st[:, 1:2], op=ALU.add), reads=["st0", "st1"], writes=["st2"])
                rstd_from(st[:, 2:3], D, EPS, "st2", "st3", st[:, 3:4])
                S.op("dve", "scalar_tensor_tensor", dict(out=tmp, in0=pa[:, :], scalar=st[:, 3:4], in1=gb[:, 0, :], op0=ALU.mult, op1=ALU.mult),
                     reads=[pk, "st3", "gb"], writes=[tk])
                S.op("pool", "tensor_tensor", dict(out=h[:, j, :], in0=h[:, j, :], in1=tmp, op=ALU.add), reads=["h", tk], writes=["h"])
                S.op("act", "activation", dict(out=tmp, in_=h[:, j, :], func=AF.Square, accum_out=st[:, 4:5]),
                     reads=["h"], writes=[tk, "st4"])
                rstd_from(st[:, 4:5], D, EPS, "st4", "st5", st[:, 5:6])
                S.op("dve", "tensor_scalar", dict(out=hn[:, j, :], in0=h[:, j, :], scalar1=st[:, 5:6], scalar2=None, op0=ALU.mult),
                     reads=["h", "st5"], writes=["hn"])
                for k in range(8):
                    S.op("pe", "transpose", dict(out=pt[:, k, :], in_=hn[:, j, k * 128:(k + 1) * 128], identity=ident[:, :]),
                         reads=["hn", "ident"], writes=["pt"])
                S.op("dve", "tensor_tensor", dict(out=hnT[:, :, j * 128:(j + 1) * 128], in0=pt[:, :, :],
                                                       in1=g2c[:, :].unsqueeze(2).broadcast_to([128, 8, 128]), op=ALU.mult),
                     reads=["pt", "g2c"], writes=["hnT"])
            for jp in range(NFF):
                s = jp % 2
                jg, jv = jp, NFF + jp
                puk = "pu%d" % s
                for which, jc in ((0, jg), (1, jv)):
                    for k in range(8):
                        S.op("pe", "matmul", dict(out=pu[s][:, which, 0:W], lhsT=wup[:, k, jc * 128:(jc + 1) * 128],
                                                                             rhs=hnT[:, k, 0:W], start=(k == 0), stop=(k == 7)),
                             reads=["wup", "hnT"], writes=[puk])
                fs = "f%d" % s
                ugs = ftmp[:, s, 0:BW + 2]; uvs = ftmp[:, s, BW + 2:2 * BW + 4]
                cg = ftmp[:, s, 2 * BW + 4:3 * BW + 4]; cv = ftmp[:, s, 3 * BW + 4:4 * BW + 4]; gel = cg
                for which, jc, ub, cb, nm in ((0, jg, ugs, cg, "g"), (1, jv, uvs, cv, "v")):
                    ku = fs + "u" + nm; kc = fs + "c" + nm
                    S.op("act", "activation", dict(out=ub[:, 2:2 + W], in_=pu[s][:, which, 0:W], func=AF.Identity),
                         reads=[puk], writes=[ku])
                    S.op("pool", "tensor_copy", dict(out=ub[:, 0:2], in_=carry[:, jc, :]), reads=["carry%d" % jc], writes=[ku + "c"])
                    if not is_halo:
                        S.op("act", "activation", dict(out=cb[:, 0:W], in_=pu[s][:, which, 0:W], func=AF.Identity,
                                                                                  scale=cp[:, jc, 2:3], bias=cp[:, jc, 3:4]),
                             reads=[puk, "cp"], writes=[kc])
                        S.op("dve", "scalar_tensor_tensor", dict(out=cb[:, 0:W], in0=ub[:, 1:1 + W], scalar=cp[:, jc, 1:2], in1=cb[:, 0:W],
                                                                                 op0=ALU.mult, op1=ALU.add),
                             reads=[ku, ku + "c", kc, "cp"], writes=[kc])
                        S.op("dve", "scalar_tensor_tensor", dict(out=cb[:, 0:W], in0=ub[:, 0:W], scalar=cp[:, jc, 0:1], in1=cb[:, 0:W],
                                                                                 op0=ALU.mult, op1=ALU.add),
                             reads=[ku, ku + "c", kc, "cp"], writes=[kc])
                    S.op("pool", "tensor_copy", dict(out=carry[:, jc, :], in_=ub[:, W:W + 2]), reads=[ku, ku + "c"], writes=["carry%d" % jc])
                if not is_halo:
                    S.op("act", "activation", dict(out=gel[:, 0:W], in_=cg[:, 0:W], func=AF.Gelu_apprx_tanh),
                         reads=[fs + "cg"], writes=[fs + "cg"])
                    S.op("pool", "tensor_tensor", dict(out=actT[:, jp, 0:W], in0=gel[:, 0:W], in1=cv[:, 0:W], op=ALU.mult),
                         reads=[fs + "cg", fs + "cv"], writes=["actT"])
            if is_halo:
                continue
            for j in range(nt):
                pa = pA[j % 2]; pk = "pA%d" % (j % 2)
                for hf in range(2):
                    for jp in range(NFF):
                        S.op("pe", "matmul", dict(out=pa[:, hf * 512:(hf + 1) * 512], lhsT=actT[:, jp, j * 128:(j + 1) * 128],
                                                              rhs=wdn[:, jp, hf * 512:(hf + 1) * 512], start=(jp == 0), stop=(jp == NFF - 1)),
                             reads=["actT", "wdn"], writes=[pk])
                oi = stg_i[0] % 2; stg_i[0] += 1
                ob = stg[oi]; ok_ = "stg%d" % oi
                junk = ob[:, 0:512]
                for hf in range(2):
                    S.op("act", "activation", dict(out=junk, in_=pa[:, hf * 512:(hf + 1) * 512], func=AF.Square, accum_out=st[:, 6 + hf:7 + hf]),
                         reads=[pk], writes=[ok_, "st%d" % (6 + hf)])
                S.op("dve", "tensor_tensor", dict(out=st[:, 8:9], in0=st[:, 6:7], in1=st[:, 7:8], op=ALU.add), reads=["st6", "st7"], writes=["st8"])
                rstd_from(st[:, 8:9], D, EPS, "st8", "st9", st[:, 9:10])
                S.op("dve", "scalar_tensor_tensor", dict(out=ob[:, :], in0=pa[:, :], scalar=st[:, 9:10], in1=gb[:, 1, :], op0=ALU.mult, op1=ALU.mult),
                     reads=[pk, "st9", "gb"], writes=[ok_])
                S.op("pool", "tensor_tensor", dict(out=ob[:, :], in0=h[:, j, :], in1=ob[:, :], op=ALU.add), reads=["h", ok_], writes=[ok_])
                r0 = t0 - ntok_halo + j * 128
                S.op("sp", "dma_start", dict(out=outd[r0:r0 + 128, :], in_=ob[:, :]), reads=["stg%d" % oi], dma=True)
        S.emit()
        pass
    return nc


def prep_A(d, c, S_tok, xT):
    f = np.float32
    h = c; dh = c // 2; p = c % 2
    w_in = d["w_in"][0]
    cols = np.concatenate([np.arange(64 * h, 64 * h + 64), 512 + np.arange(64 * h, 64 * h + 64), 1024 + np.arange(64 * h, 64 * h + 64),
                           np.arange(1536, 1600), np.arange(1600, 1664), np.arange(1664, 1792),
                           1792 + dh * 128 + np.arange(128), 1792 + 512 + dh * 128 + np.arange(128), 1792 + 1024 + dh * 128 + np.arange(128)])
    wA = np.ascontiguousarray(w_in[:, cols])
    mu = d["mu_shift"][0]
    hs = slice(64 * h, 64 * h + 64)
    pv64 = np.zeros((64, 16), f)
    for i in range(5):
        pv64[:, i] = mu[cols[64 * i:64 * i + 64]]
    pv64[:, 5] = d["w_decay0"][0][hs]; pv64[:, 6] = d["w_iclr0"][0][hs]; pv64[:, 7] = d["k_k"][0][hs]; pv64[:, 8] = d["k_a"][0][hs]
    pv64[:, 9] = d["r_k"][0][h]
    pv128 = np.zeros((128, 4), f); pv128[:, 0] = mu[1664:1792]; pv128[:, 1] = d["diff_subln"][0]
    bc64 = np.zeros((128, 3, 64), f); bc64[:, 0] = d["ln_x_w"][0][hs][None]; bc64[:, 1] = d["ln_x_b"][0][hs][None]
    lamv = np.stack([d["lambda_q1"][0], d["lambda_k1"][0], d["lambda_q2"][0], d["lambda_k2"][0]])[None].astype(f)
    lamv = np.ascontiguousarray(np.broadcast_to(lamv, (128, 4, 64)))
    i = np.arange(128)[:, None]; t = np.arange(128)[None, :]
    same = (i // 64) == (t // 64)
    masks = np.stack([same & (i < t), same & (i <= t), same & (i < t), same & (i <= t), same & (t < i)], 1).astype(f)
    slope = 2.0 ** (-8.0 * (dh + 1) / 4)
    kl = np.arange(128)[:, None].astype(np.float64); n1 = np.arange(136)[None, :].astype(np.float64)
    abias = (slope * (kl - 128.0 * (n1 - 7))).astype(f)
    maskm = np.zeros((128, 8, 4, 128), f)
    klq = np.arange(128)[:, None]; qlq = np.arange(128)[None, :]
    for jr in range(8):
        for ib in range(4):
            tq = p + 2 * ib
            if jr > tq:
                maskm[:, jr, ib, :] = -30000.0
            elif jr == tq:
                maskm[:, jr, ib, :] = np.where(klq <= qlq, 0.0, -30000.0)
    maskm = np.ascontiguousarray(maskm.reshape(128, 8, 512))
    qaug = np.repeat(np.array([-slope * 1024.0 * (p + 2 * ib) for ib in range(4)], f), 128)[None, :].astype(f)
    psel = np.ascontiguousarray(np.broadcast_to(np.array([1.0 - p, p], f)[None], (128, 2)))
    g1c = np.ascontiguousarray(d["ln_attn_pre"][0].reshape(8, 128).T)
    return {"xT": xT, "wA": wA, "g1c": g1c, "pv64": pv64, "pv128": pv128,
            "wdu": np.ascontiguousarray(d["w_decay_up"][0][:, hs]), "wiu": np.ascontiguousarray(d["w_iclr_up"][0][:, hs]),
            "wgu": np.ascontiguousarray(d["w_gate_up"][0][:, hs]), "bc64": bc64, "lamv": lamv,
            "masks": np.ascontiguousarray(masks), "maskm": maskm, "qaug": qaug, "abias": abias, "psel": psel}


def kernel(**inputs):
    d = {k: np.asarray(v) for k, v in inputs.items()}
    S_tok = 16384
    x = d["x"][0]
    xT = np.ascontiguousarray(x.T)
    ncA = build_A(S_tok)
    insA = [prep_A(d, c, S_tok, xT) for c in range(8)]
    resA = run_bass_kernel_spmd(ncA, insA, core_ids=list(range(8)))
    mix = np.empty((S_tok, 1024), np.float32)
    for c in range(8):
        mix[:, 64 * c:64 * c + 64] = resA.results[c]["y_rwkv"]
        o = resA.results[c]["oT_diff"].T.reshape(S_tok // 256, 128, 128)
        dh, p = c // 2, c % 2
        mix[:, 512 + 128 * dh:512 + 128 * dh + 128].reshape(S_tok // 128, 128, 128)[p::2] = o
    ncB = build_B()
    def bcast(v):
        return np.broadcast_to(v[None, :], (128, 1024))
    gb = np.ascontiguousarray(np.stack([bcast(d["ln_attn_post"][0]), bcast(d["ln_ffn_post"][0])], 1)).astype(np.float32)
    g2c = np.ascontiguousarray(d["ln_ffn_pre"][0].reshape(8, 128).T)
    convp = np.ascontiguousarray(np.concatenate([d["conv_w"][0], d["conv_b"][0][None]], 0).reshape(4, 44, 128).transpose(2, 1, 0))
    insB = []
    for c in range(8):
        lo = 2048 * c - 128
        if c == 0:
            m = np.concatenate([np.zeros((128, 1024), np.float32), mix[0:2048]], 0)
            xx = np.concatenate([np.zeros((128, 1024), np.float32), x[0:2048]], 0)
        else:
            m = mix[lo:lo + 2176]; xx = x[lo:lo + 2176]
        insB.append({"mixT": np.ascontiguousarray(m.T), "xr": np.ascontiguousarray(xx), "w_out": d["w_out"][0], "w_up": d["w_up"][0],
                     "w_down": d["w_down"][0], "gb": gb, "g2c": g2c, "convp": convp})
    resB = run_bass_kernel_spmd(ncB, insB, core_ids=list(range(8)))
    out = np.concatenate([r["out"] for r in resB.results], 0)[None]
    return out.astype(np.float32)
```
